# Optimizing a Trainium2 kernel written in Bass

```python
import math
import jax, jax.numpy as jnp
from jax import lax
import numpy as np

D_MODEL = 1024
BATCH = 8
SEQ = 8192
DEPTH = 1
DEC_BATCH = 8
DEC_SEQ = 64
PAST_LEN = 2048

CHUNK = 64
QBLOCK = 128
HEAD_DIM = 64
SB_HEADS = 8
SB_WIDTH = SB_HEADS * HEAD_DIM
CONV_CH = 512
CONV_WIDTH = 31
MIX_WIDTH = CONV_CH + SB_WIDTH
IN_WIDTH = 2 * CONV_CH + 3 * SB_WIDTH
D_FF = 2816
PLE_DIM = 256
EPS = 1e-6
SB_SCALE = 1.0 / math.sqrt(HEAD_DIM)

kernel_name = "hybrid_conformer_conv_stickbreaking_step"


def rms_norm(x, g):
    xf = x.astype(jnp.float32)
    y = xf * lax.rsqrt(jnp.mean(xf * xf, axis=-1, keepdims=True) + EPS)
    return (y * g.astype(jnp.float32)).astype(x.dtype)


def layer_norm(x, g, b):
    xf = x.astype(jnp.float32)
    mu = jnp.mean(xf, axis=-1, keepdims=True)
    xc = xf - mu
    y = xc * lax.rsqrt(jnp.mean(xc * xc, axis=-1, keepdims=True) + EPS)
    return (y * g.astype(jnp.float32) + b.astype(jnp.float32)).astype(x.dtype)


def swiglu(x, w_gu, w_down):
    g, u = jnp.split(x @ w_gu, 2, axis=-1)
    return (jax.nn.silu(g) * u) @ w_down


def conv_module(a, hist, w_dw, b_dw, ln_g, ln_b):
    a1, a2 = jnp.split(a, 2, axis=-1)
    u = a1 * jax.nn.sigmoid(a2)
    upad = jnp.concatenate([hist.astype(u.dtype), u], axis=1)
    y = lax.conv_general_dilated(
        upad, w_dw[:, None, :].astype(u.dtype), window_strides=(1,), padding='VALID',
        dimension_numbers=('NWC', 'WIO', 'NWC'), feature_group_count=CONV_CH) + b_dw
    y = jax.nn.silu(layer_norm(y, ln_g, ln_b))
    new_hist = upad[:, -(CONV_WIDTH - 1):]
    return y, new_hist


def sb_attend(q, k, v, q_pos, k_pos):
    z = jnp.einsum('bqhd,bkhd->bhqk', q.astype(jnp.float32), k.astype(jnp.float32)) * SB_SCALE
    mask = k_pos[None, :] < q_pos[:, None]
    log_keep = jnp.where(mask, jax.nn.log_sigmoid(-z), 0.0)
    later = lax.cumsum(log_keep, axis=3, reverse=True) - log_keep
    w = jnp.where(mask, jnp.exp(jax.nn.log_sigmoid(z) + later), 0.0)
    return jnp.einsum('bhqk,bkhd->bqhd', w.astype(v.dtype), v)


def sb_blocked(q, k, v, q_pos, k_pos):
    b, t, h, d = q.shape
    nb = t // QBLOCK
    qb = q.reshape(b, nb, QBLOCK, h, d).transpose(1, 0, 2, 3, 4)
    pb = q_pos.reshape(nb, QBLOCK)
    out = lax.map(lambda xs: sb_attend(xs[0], k, v, xs[1], k_pos), (qb, pb))
    return out.transpose(1, 0, 2, 3, 4).reshape(b, t, h, d)


def trunk_layer(h, p, k_hist, v_hist, conv_hist, q_pos, k_pos, blocked,
                ffn1_norm, ffn1_w_gu, ffn1_w_down, mix_norm, w_in,
                conv_w, conv_b, conv_ln_g, conv_ln_b, w_out,
                ffn2_norm, ffn2_w_gu, ffn2_w_down, ple_norm, ple_gate_w, ple_w):
    b, l, _ = h.shape
    h = h + 0.5 * swiglu(rms_norm(h, ffn1_norm), ffn1_w_gu, ffn1_w_down)
    proj = rms_norm(h, mix_norm) @ w_in
    a_conv = proj[..., :2 * CONV_CH]
    qkv = proj[..., 2 * CONV_CH:].reshape(b, l, 3, SB_HEADS, HEAD_DIM)
    q, k, v = qkv[:, :, 0], qkv[:, :, 1], qkv[:, :, 2]
    conv_out, new_conv = conv_module(a_conv, conv_hist, conv_w, conv_b, conv_ln_g, conv_ln_b)
    if k_hist is None:
        k_all, v_all = k, v
    else:
        k_all = jnp.concatenate([k_hist.astype(k.dtype), k], axis=1)
        v_all = jnp.concatenate([v_hist.astype(v.dtype), v], axis=1)
    if blocked:
        sb = sb_blocked(q, k_all, v_all, q_pos, k_pos)
    else:
        sb = sb_attend(q, k_all, v_all, q_pos, k_pos)
    mixed = jnp.concatenate([conv_out, sb.reshape(b, l, SB_WIDTH)], axis=-1) @ w_out
    h = h + mixed
    h = h + 0.5 * swiglu(rms_norm(h, ffn2_norm), ffn2_w_gu, ffn2_w_down)
    gate = jax.nn.sigmoid(rms_norm(h, ple_norm) @ ple_gate_w)
    h = h + gate * (p @ ple_w)
    return h, k, v, new_conv


def setup_inputs(seed: int = 0) -> dict:
    key = jax.random.key(seed)
    ks = jax.random.split(key, 32)
    f32 = jnp.float32

    def nrm(k, shape, scale):
        return jax.random.normal(k, shape, f32) * scale

    def gain(k, shape):
        return 1.0 + 0.05 * jax.random.normal(k, shape, f32)

    return {
        "x_prompt": nrm(ks[0], (BATCH, SEQ, D_MODEL), 1.0),
        "x_sample": nrm(ks[1], (DEC_BATCH, DEC_SEQ, D_MODEL), 1.0),
        "p_prompt": nrm(ks[2], (DEPTH, BATCH, SEQ, PLE_DIM), 1.0),
        "p_sample": nrm(ks[3], (DEPTH, DEC_BATCH, DEC_SEQ, PLE_DIM), 1.0),
        "cache_k": nrm(ks[4], (DEPTH, DEC_BATCH, PAST_LEN, SB_HEADS, HEAD_DIM), 1.0),
        "cache_v": nrm(ks[5], (DEPTH, DEC_BATCH, PAST_LEN, SB_HEADS, HEAD_DIM), 1.0),
        "state_conv": nrm(ks[6], (DEPTH, DEC_BATCH, CONV_WIDTH - 1, CONV_CH), 0.5),
        "ffn1_norm": gain(ks[7], (DEPTH, D_MODEL)),
        "ffn1_w_gu": nrm(ks[8], (DEPTH, D_MODEL, 2 * D_FF), D_MODEL ** -0.5),
        "ffn1_w_down": nrm(ks[9], (DEPTH, D_FF, D_MODEL), D_FF ** -0.5),
        "mix_norm": gain(ks[10], (DEPTH, D_MODEL)),
        "w_in": nrm(ks[11], (DEPTH, D_MODEL, IN_WIDTH), D_MODEL ** -0.5),
        "conv_w": nrm(ks[12], (DEPTH, CONV_WIDTH, CONV_CH), CONV_WIDTH ** -0.5),
        "conv_b": nrm(ks[13], (DEPTH, CONV_CH), 0.02),
        "conv_ln_g": gain(ks[14], (DEPTH, CONV_CH)),
        "conv_ln_b": nrm(ks[15], (DEPTH, CONV_CH), 0.02),
        "w_out": nrm(ks[16], (DEPTH, MIX_WIDTH, D_MODEL), MIX_WIDTH ** -0.5),
        "ffn2_norm": gain(ks[17], (DEPTH, D_MODEL)),
        "ffn2_w_gu": nrm(ks[18], (DEPTH, D_MODEL, 2 * D_FF), D_MODEL ** -0.5),
        "ffn2_w_down": nrm(ks[19], (DEPTH, D_FF, D_MODEL), D_FF ** -0.5),
        "ple_norm": gain(ks[20], (DEPTH, D_MODEL)),
        "ple_gate_w": nrm(ks[21], (DEPTH, D_MODEL, D_MODEL), D_MODEL ** -0.5),
        "ple_w": nrm(ks[22], (DEPTH, PLE_DIM, D_MODEL), PLE_DIM ** -0.5),
        "final_norm": gain(ks[23], (D_MODEL,)),
    }


def reference(x_prompt, x_sample, p_prompt, p_sample, cache_k, cache_v, state_conv,
              ffn1_norm, ffn1_w_gu, ffn1_w_down, mix_norm, w_in,
              conv_w, conv_b, conv_ln_g, conv_ln_b, w_out,
              ffn2_norm, ffn2_w_gu, ffn2_w_down, ple_norm, ple_gate_w, ple_w, final_norm):
    b_p, t_p, _ = x_prompt.shape
    b_s, t_s, _ = x_sample.shape
    past = cache_k.shape[2]
    pos_prompt = jnp.arange(t_p, dtype=jnp.int32)
    q_pos_sample = past + jnp.arange(t_s, dtype=jnp.int32)
    k_pos_sample = jnp.arange(past + t_s, dtype=jnp.int32)

    h_p, h_s = x_prompt, x_sample
    kp_list, vp_list, cp_list, ks_list, vs_list, cs_list = [], [], [], [], [], []
    for i in range(DEPTH):
        lw = (ffn1_norm[i], ffn1_w_gu[i], ffn1_w_down[i], mix_norm[i], w_in[i],
              conv_w[i], conv_b[i], conv_ln_g[i], conv_ln_b[i], w_out[i],
              ffn2_norm[i], ffn2_w_gu[i], ffn2_w_down[i], ple_norm[i], ple_gate_w[i], ple_w[i])
        conv0 = jnp.zeros((b_p, CONV_WIDTH - 1, CONV_CH), x_prompt.dtype)
        h_p, k_p, v_p, c_p = trunk_layer(h_p, p_prompt[i], None, None, conv0,
                                         pos_prompt, pos_prompt, True, *lw)
        h_s, k_s, v_s, c_s = trunk_layer(h_s, p_sample[i], cache_k[i], cache_v[i], state_conv[i],
                                         q_pos_sample, k_pos_sample, False, *lw)
        kp_list.append(k_p); vp_list.append(v_p); cp_list.append(c_p)
        ks_list.append(k_s); vs_list.append(v_s); cs_list.append(c_s)

    y_prompt = rms_norm(h_p, final_norm)
    y_sample = rms_norm(h_s, final_norm)
    new_k_prompt = jnp.stack(kp_list)
    new_v_prompt = jnp.stack(vp_list)
    new_conv_prompt = jnp.stack(cp_list)
    new_k_sample = jnp.stack(ks_list)
    new_v_sample = jnp.stack(vs_list)
    new_conv_sample = jnp.stack(cs_list)
    return (y_prompt, y_sample, new_k_prompt, new_v_prompt, new_conv_prompt,
            new_k_sample, new_v_sample, new_conv_sample)
```

```python
import numpy as np
from contextlib import ExitStack
import concourse.bass as bass
import concourse.mybir as mybir
from concourse.bass_utils import run_bass_kernel_spmd

F32 = mybir.dt.float32
BF16 = mybir.dt.bfloat16
ALU = mybir.AluOpType
AF = mybir.ActivationFunctionType

D = 1024
DFF = 2816
SEQ = 8192
TS = 64
PAST = 2048
EPS = 1e-6
NSLAB = 48
RECIP_ACT_MAX_KB = 9999
NPAR = 176
ENGS = ("pe", "act", "dve", "pool", "sp")


class Ins:
    __slots__ = ("eng", "fn", "deps", "signal", "val", "chan")


class Prog:
    def __init__(self):
        self.streams = {e: [] for e in ENGS}
        self.lastw = {}
        self.readers = {}
        self.chan_count = {}

    def _need(self, p, c, raw):
        if p.chan is not None:
            return True
        if c.chan is None and p.eng == c.eng:
            if p.eng == "pool":
                return True
            return raw and p.eng != "pe"
        return True

    def add(self, eng, fn, reads=(), writes=(), chan=None):
        c = Ins()
        c.eng, c.fn, c.chan, c.signal, c.val = eng, fn, chan, chan is not None, 0
        deps = {}
        for r in reads:
            p = self.lastw.get(r)
            if p is not None and self._need(p, c, True):
                deps[id(p)] = p
        for w in writes:
            p = self.lastw.get(w)
            if p is not None and self._need(p, c, False):
                deps[id(p)] = p
            for p in self.readers.get(w, {}).values():
                if self._need(p, c, False):
                    deps[id(p)] = p
        c.deps = list(deps.values())
        for p in c.deps:
            p.signal = True
        for w in writes:
            self.lastw[w] = c
            self.readers[w] = {}
        for r in reads:
            if r not in writes:
                self.readers.setdefault(r, {})[("ch", chan) if chan is not None else eng] = c
        self.streams[eng].append(c)
        return c

    def emit(self, nc, stack):
        chans = set()
        for e in ENGS:
            for c in self.streams[e]:
                if c.chan is not None:
                    chans.add(c.chan)
        sem = {e: stack.enter_context(nc.semaphore("s_" + e)) for e in ENGS}
        for ch in sorted(chans):
            sem[("ch", ch)] = stack.enter_context(nc.semaphore("c_" + ch))
        cnt = {}
        for e in ENGS:
            for c in self.streams[e]:
                if c.chan is not None:
                    k = ("ch", c.chan)
                    cnt[k] = cnt.get(k, 0) + 16
                    c.val = cnt[k]
                elif c.signal:
                    cnt[e] = cnt.get(e, 0) + 1
                    c.val = cnt[e]
        self.maxvals = dict(cnt)
        final = {k: v for k, v in cnt.items() if isinstance(k, tuple)}

        def run(e, handle):
            seen = {}
            for c in self.streams[e]:
                for p in c.deps:
                    k = ("ch", p.chan) if p.chan is not None else p.eng
                    if seen.get(k, 0) < p.val:
                        handle.wait_ge(sem[k], p.val)
                        seen[k] = p.val
                ins = c.fn(handle)
                if c.chan is not None:
                    ins.then_inc(sem[("ch", c.chan)], 16)
                elif c.signal:
                    ins.then_inc(sem[e], 1)
            if e == "sp":
                for k, v in final.items():
                    if seen.get(k, 0) < v:
                        handle.wait_ge(sem[k], v)

        with nc.Block() as block:
            @block.tensor
            def _(h):
                run("pe", h)

            @block.scalar
            def _(h):
                run("act", h)

            @block.vector
            def _(h):
                run("dve", h)

            @block.gpsimd
            def _(h):
                run("pool", h)

            @block.sync
            def _(h):
                run("sp", h)


def slab_defs():
    defs = []

    def ffn(pfx):
        for j in range(6):
            defs.append((pfx + "gu", 8, 512, [(0, 256, 0, 256 * j), (256, 256, 0, DFF + 256 * j)]))
        for ob in range(4):
            defs.append((pfx + "dn", 12, 256, [(0, 256, 0, 256 * ob)]))
        for j in range(6, 11):
            defs.append((pfx + "gu", 8, 512, [(0, 256, 0, 256 * j), (256, 256, 0, DFF + 256 * j)]))
        for ob in range(4):
            defs.append((pfx + "dn", 10, 256, [(0, 256, 12, 256 * ob)]))

    ffn("f1")
    for c in range(5):
        defs.append(("win", 8, 512, [(0, 512, 0, 512 * c)]))
    for o in range(2):
        defs.append(("wout", 8, 512, [(0, 512, 0, 512 * o)]))
    ffn("f2")
    for o in range(2):
        defs.append(("pgate", 8, 512, [(0, 512, 0, 512 * o)]))
    defs.append(("plew", 2, 1024, [(0, 1024, 0, 0)]))
    assert len(defs) == NSLAB
    return defs


class DummyProg:
    def add(self, *a, **k):
        return None


def build_nc(n_ptiles=16, do_sample=True, pipelined=True):
    nc = bass.Bass("TRN2", target_bir_lowering=False)
    PH = [Prog()]
    stack = ExitStack()

    def ADD(*a, **k):
        return PH[0].add(*a, **k)

    def din(name, shape):
        return nc.dram_tensor(name, list(shape), F32, kind="ExternalInput").ap()

    def dout(name, shape):
        return nc.dram_tensor(name, list(shape), F32, kind="ExternalOutput").ap()

    x_p = din("x_p", [SEQ, D]); x_s = din("x_s", [TS, D])
    p_p = din("p_p", [SEQ, 256]); p_s = din("p_s", [TS, 256])
    ck = din("ck", [PAST, 512]); cv = din("cv", [PAST, 512]); sc = din("sc", [30, 512])
    W = {
        "f1gu": din("f1gu", [D, 2 * DFF]), "f1dn": din("f1dn", [DFF, D]),
        "win": din("win", [D, 2560]), "wout": din("wout", [D, D]),
        "f2gu": din("f2gu", [D, 2 * DFF]), "f2dn": din("f2dn", [DFF, D]),
        "pgate": din("pgate", [D, D]), "plew": din("plew", [256, D]),
    }
    params_d = din("params", [128, NPAR])
    consts_d = din("consts", [128, 512])
    y_p = dout("y_p", [SEQ, D]); y_s = dout("y_s", [TS, D])
    nk_p = dout("nk_p", [SEQ, 512]); nv_p = dout("nv_p", [SEQ, 512]); nc_p = dout("nc_p", [30, 512])
    nk_s = dout("nk_s", [TS, 512]); nv_s = dout("nv_s", [TS, 512]); nc_s = dout("nc_s", [30, 512])

    wsc = nc.dram_tensor("wsc", [NSLAB, 128, 4096], BF16).ap()
    ktscr = {"p": nc.dram_tensor("ktscr_p", [4, 128, SEQ], BF16).ap(),
             "s": nc.dram_tensor("ktscr_s", [4, 128, PAST + 128], BF16).ap()}
    vscr = {"p": nc.dram_tensor("vscr_p", [4, 128, 64, 128], BF16).ap(),
            "s": nc.dram_tensor("vscr_s", [4, 128, 17, 128], BF16).ap()}

    def sb(name, shape, dt):
        return stack.enter_context(nc.sbuf_tensor(name, list(shape), dt))

    def pst(name):
        return stack.enter_context(nc.psum_tensor(name, [128, 512], F32))

    ktc = [sb("ktc%d" % i, [128, 2048], BF16) for i in range(2)]
    vc = [sb("vc%d" % i, [128, 2048], BF16) for i in range(2)]
    hTs = [sb("hT%d" % i, [128, 8, 512], F32) for i in range(2)]
    mixTs = [sb("mixT%d" % i, [128, 8, 512], BF16) for i in range(2)]
    QTs = [sb("QT%d" % i, [128, 4, 512], BF16) for i in range(2)]
    xn = sb("xn", [128, 8, 512], BF16)
    act = sb("act", [128, 12, 512], BF16)
    KTt = sb("KTt", [128, 4, 512], BF16)
    Vt = sb("Vt", [128, 4, 512], BF16)
    u = sb("u", [128, 4, 542], F32)
    ub = sb("ub", [128, 4, 542], BF16)
    dg = [sb("dg%d" % i, [128, 128], BF16) for i in range(4)]
    wslab = [sb("wslab%d" % i, [128, 4096], BF16) for i in range(3)]
    tmpf = [sb("tmpf%d" % i, [128, 2, 512], F32) for i in range(3)]
    spb = [sb("spb%d" % i, [128, 2, 512], BF16) for i in range(2)]
    xxb = [sb("xxb%d" % i, [128, 2, 512], BF16) for i in range(2)]
    wwb = [sb("wwb%d" % i, [128, 2, 512], BF16) for i in range(2)]
    sgt = [sb("sgt%d" % i, [128, 512], F32) for i in range(2)]
    eet = [sb("eet%d" % i, [128, 512], F32) for i in range(2)]
    lnm = sb("lnm", [128, 512], F32)
    lnv = sb("lnv", [128, 512], F32)
    rstd = sb("rstd", [128, 512], F32)
    xs = [sb("xs%d" % i, [128, 1024], F32) for i in range(2)]
    ys = [sb("ys%d" % i, [128, 1024], F32) for i in range(2)]
    pstg = [sb("pstg%d" % i, [128, 256], F32) for i in range(2)]
    pT = sb("pT", [128, 2, 512], BF16)
    kvo = [sb("kvo%d" % i, [128, 512], F32) for i in range(2)]
    cst = sb("cst", [128, 512], F32)
    cstb = sb("cstb", [128, 512], BF16)
    par = sb("par", [128, NPAR], F32)
    sc_sb = sb("sc_sb", [32, 512], F32)
    cvo = sb("cvo", [32, 512], F32)

    psZ = stack.enter_context(nc.psum_tensor("psZ", [128, 1024], F32))
    psC = stack.enter_context(nc.psum_tensor("psC", [128, 1024], F32))
    PS = {n: pst("ps" + n) for n in ("O", "G", "U", "X")}
    psZ3 = psZ[:].rearrange("p (h q) -> p h q", h=2)
    psC3 = psC[:].rearrange("p (h q) -> p h q", h=2)
    ident = cst[:, 0:128]
    maskM = cst[:, 384:512]
    onesb = cstb[:, 0:128]
    unegb = cstb[:, 128:256]
    lnegb = cstb[:, 256:384]
    identb = cstb[:, 384:512]
    yacc = act[:, 0:8, :].rearrange("p a b -> p (a b)").bitcast(F32).rearrange("p (a b) -> p a b", a=4)
    ctx = {"slab": 0, "xs": 0, "ys": 0, "ps": 0, "kvo": 0, "chunk": 0, "recip_act": True, "dg": 0}

    ADD("sp", lambda e: e.dma_start(out=cst[:], in_=consts_d[:, :]), writes=[("cst",)], chan="cst")
    ADD("sp", lambda e: e.dma_start(out=par[:], in_=params_d[:, :]), writes=[("par",)], chan="par")
    ADD("dve", lambda e: e.tensor_copy(out=cstb[:, 128:384], in_=cst[:, 128:384]), reads=[("cst",)], writes=[("cstb",)])
    ADD("dve", lambda e: e.tensor_copy(out=cstb[:, 384:512], in_=cst[:, 0:128]), reads=[("cst",)], writes=[("cstb",)])
    ADD("dve", lambda e: e.memset(cstb[:, 0:128], 1.0), writes=[("cstb",)])

    def recip1p(ei, T):
        if ctx["recip_act"]:
            ADD("act", lambda e: e.activation(out=eet[ei][:, 0:T], in_=eet[ei][:, 0:T], func=AF.Ln, bias=1.0),
                reads=[("eet", ei)], writes=[("eet", ei)])
            ADD("act", lambda e: e.activation(out=eet[ei][:, 0:T], in_=eet[ei][:, 0:T], func=AF.Exp, scale=-1.0),
                reads=[("eet", ei)], writes=[("eet", ei)])
        else:
            ADD("dve", lambda e: e.tensor_scalar(out=eet[ei][:, 0:T], in0=eet[ei][:, 0:T], scalar1=1.0, scalar2=None, op0=ALU.add),
                reads=[("eet", ei)], writes=[("eet", ei)])
            ADD("dve", lambda e: e.reciprocal(out=eet[ei][:, 0:T], in_=eet[ei][:, 0:T]),
                reads=[("eet", ei)], writes=[("eet", ei)])
    for b in range(2):
        ADD("pool", lambda e, b=b: e.memset(xs[b][:], 0.0), writes=[("xs", b)])
        ADD("pool", lambda e, b=b: e.memset(pstg[b][:], 0.0), writes=[("pstg", b)])

    defs = slab_defs()
    for sid, (wn, KC, Wd, pieces) in enumerate(defs):
        b = sid % 2
        stg = hTs[b][:].rearrange("p a b -> p (a b)")
        stb = mixTs[b][:].rearrange("p a b -> p (a b)")
        wv = W[wn].rearrange("(k p) n -> p k n", p=128)
        n = KC * Wd
        dst3 = stg[:, 0:n].rearrange("p (k w) -> p k w", k=KC)
        for pi, (wo, wl, k0, c0) in enumerate(pieces):
            ADD("sp", lambda e, dst3=dst3, wv=wv, wo=wo, wl=wl, k0=k0, c0=c0, KC=KC:
                e.dma_start(out=dst3[:, :, wo:wo + wl], in_=wv[:, k0:k0 + KC, c0:c0 + wl]),
                writes=[("hT", b, "pre", pi)], chan="pre_l%d_%d" % (b, pi))
        eng = "dve" if sid % 2 == 0 else "pool"
        ADD(eng, lambda e, stb=stb, stg=stg, n=n: e.tensor_copy(out=stb[:, 0:n], in_=stg[:, 0:n]),
            reads=[("hT", b, "pre", 0), ("hT", b, "pre", 1)], writes=[("mixT", b, "pre")])
        ADD("pool", lambda e, stb=stb, sid=sid, n=n: e.dma_start(out=wsc[sid][:, 0:n], in_=stb[:, 0:n]),
            reads=[("mixT", b, "pre")], writes=[("wsc", sid)], chan="pre_s%d" % b)
    for b in range(2):
        ADD("pool", lambda e, b=b: e.memset(mixTs[b][:, 0, 0:8], 0.0),
            reads=[("hT", b, "pre", 0), ("hT", b, "pre", 1), ("mixT", b, "pre")],
            writes=[("hT", b, dc) for dc in range(8)] + [("mixT", b, dc) for dc in range(8)] + [("mixT", b, "pre")])

    def get_slab(sid):
        b = ctx["slab"] % 3
        ctx["slab"] += 1
        KC, Wd = defs[sid][1], defs[sid][2]
        n = KC * Wd
        ADD("sp", lambda e: e.dma_start(out=wslab[b][:, 0:n], in_=wsc[sid][:, 0:n]),
            reads=[("wsc", sid)], writes=[("wslab", b)], chan="wslab%d" % b)
        return b, wslab[b][:, 0:n].rearrange("p (k w) -> p k w", k=KC)

    def mm(out, lhsT, rhs, start, stop, reads, writes):
        ADD("pe", lambda e: e.matmul(out, lhsT, rhs, start=start, stop=stop, skip_group_check=True),
            reads=reads, writes=writes)

    def rmsnorm(T, hp, gcol0, final=False):
        hT = hTs[hp]
        for dc in range(8):
            eng = "dve"
            ADD(eng, lambda e, dc=dc: e.tensor_tensor(out=act[:, dc, 0:T], in0=hT[:, dc, 0:T], in1=hT[:, dc, 0:T], op=ALU.mult),
                reads=[("hT", hp, dc)], writes=[("act", dc)])
        for dc in range(8):
            mm(PS["X"][:, 0:T], onesb, act[:, dc, 0:T], dc == 0, dc == 7,
               reads=[("act", dc), ("cstb",)], writes=[("ps", "X")])
        ADD("act", lambda e: e.activation(out=rstd[:, 0:T], in_=PS["X"][:, 0:T], func=AF.Ln, bias=EPS, scale=1.0 / D),
            reads=[("ps", "X")], writes=[("rstd",)])
        ADD("act", lambda e: e.activation(out=rstd[:, 0:T], in_=rstd[:, 0:T], func=AF.Exp, scale=-0.5),
            reads=[("rstd",)], writes=[("rstd",)])
        for dc in range(8):
            if final:
                ADD("dve", lambda e, dc=dc: e.scalar_tensor_tensor(out=hT[:, dc, 0:T], in0=hT[:, dc, 0:T], scalar=par[:, gcol0 + dc:gcol0 + dc + 1],
                                                                    in1=rstd[:, 0:T], op0=ALU.mult, op1=ALU.mult),
                    reads=[("hT", hp, dc), ("rstd",), ("par",)], writes=[("hT", hp, dc)])
            else:
                ADD("dve", lambda e, dc=dc: e.scalar_tensor_tensor(out=xn[:, dc, 0:T], in0=hT[:, dc, 0:T], scalar=par[:, gcol0 + dc:gcol0 + dc + 1],
                                                                    in1=rstd[:, 0:T], op0=ALU.mult, op1=ALU.mult),
                    reads=[("hT", hp, dc), ("rstd",), ("par",)], writes=[("xn", dc)])

    def ffn(T, hp, sid0):
        hT = hTs[hp]
        sid = sid0
        fcg = 0
        w16 = 16 * 0.22 * T / 512
        for half, (ngu, nfc) in enumerate(((6, 12), (5, 10))):
            for j in range(ngu):
                b, sl = get_slab(sid); sid += 1
                for sub in range(2):
                    fcl = 2 * j + sub
                    for kc in range(8):
                        mm(PS["G"][:, 0:T], sl[:, kc, sub * 128:(sub + 1) * 128], xn[:, kc, 0:T], kc == 0, kc == 7,
                           reads=[("wslab", b), ("xn", kc)], writes=[("ps", "G")])
                        if kc % 4 == 3:
                            yield 4 * 0.22 * T / 512
                    tb = fcg % 2
                    ADD("act", lambda e, tb=tb: e.activation(out=eet[tb][:, 0:T], in_=PS["G"][:, 0:T], func=AF.Exp, scale=-1.0),
                        reads=[("ps", "G")], writes=[("eet", tb), ("gread", tb)])
                    ADD("dve", lambda e, tb=tb: e.tensor_copy(out=sgt[tb][:, 0:T], in_=PS["G"][:, 0:T]),
                        reads=[("ps", "G"), ("gread", tb)], writes=[("sgt", tb)])
                    for kc in range(8):
                        mm(PS["U"][:, 0:T], sl[:, kc, 256 + sub * 128:256 + (sub + 1) * 128], xn[:, kc, 0:T], kc == 0, kc == 7,
                           reads=[("wslab", b), ("xn", kc)], writes=[("ps", "U")])
                        if kc % 4 == 3:
                            yield 4 * 0.22 * T / 512
                    ADD("dve", lambda e, tb=tb: e.tensor_tensor(out=sgt[tb][:, 0:T], in0=sgt[tb][:, 0:T], in1=PS["U"][:, 0:T], op=ALU.mult),
                        reads=[("sgt", tb), ("ps", "U")], writes=[("sgt", tb)])
                    recip1p(tb, T)
                    ADD("dve", lambda e, tb=tb, fcl=fcl: e.tensor_tensor(out=act[:, fcl, 0:T], in0=sgt[tb][:, 0:T], in1=eet[tb][:, 0:T], op=ALU.mult),
                        reads=[("sgt", tb), ("eet", tb)], writes=[("act", fcl)])
                    fcg += 1
                    yield w16
            for ob in range(4):
                b, sl = get_slab(sid); sid += 1
                for o2 in range(2):
                    oc = 2 * ob + o2
                    pX = ("X", "G", "U")[oc % 3]
                    for fc in range(nfc):
                        mm(PS[pX][:, 0:T], sl[:, fc, o2 * 128:(o2 + 1) * 128], act[:, fc, 0:T], fc == 0, fc == nfc - 1,
                           reads=[("wslab", b), ("act", fc)], writes=[("ps", pX)])
                        if fc % 4 == 3:
                            yield 4 * 0.22 * T / 512
                    ADD("dve", lambda e, oc=oc, pX=pX: e.scalar_tensor_tensor(out=hT[:, oc, 0:T], in0=PS[pX][:, 0:T], scalar=0.5, in1=hT[:, oc, 0:T],
                                                                              op0=ALU.mult, op1=ALU.add),
                        reads=[("ps", pX), ("hT", hp, oc)], writes=[("hT", hp, oc)])
                    yield nfc * 0.22 * T / 512
        assert sid == sid0 + 19

    def kv_store(which, key0, T):
        nq = T // 128
        g = key0 // 512
        kdst = ktscr[which].rearrange("c p k -> p c k")
        ADD("sp", lambda e: e.dma_start(out=kdst[:, :, key0:key0 + T], in_=KTt[:, :, 0:T]),
            reads=[("KTt",)], writes=[("kts", which, g)], chan="kts")
        b0 = key0 // 128
        for c in range(4):
            ADD("sp", lambda e, c=c: e.dma_start(out=vscr[which][c][:, b0:b0 + nq, :], in_=Vt[:, 0:nq, c * 128:(c + 1) * 128]),
                reads=[("Vt",)], writes=[("vts", which, g)], chan="vts")

    def genB(t):
        which, T, kb0, tp = t["which"], t["T"], t["kb0"], t["par"]
        QT, mixT = QTs[tp], mixTs[tp]
        nq = T // 128
        nkb = kb0 + nq
        ZB = ("Z0", "Z1")
        CB = ("C0", "C1")
        for c in range(4):
            order = list(range(nkb - 1, -1, -1))
            nst = len(order)
            chbuf = {}

            def load_chunk(ch, c=c):
                b = ctx["chunk"] % 2
                ctx["chunk"] += 1
                k0 = ch * 16
                k1 = min(nkb, k0 + 16)
                nb = k1 - k0
                gs = list(range(k0 // 4, (k1 - 1) // 4 + 1))
                ADD("pool", lambda e: e.dma_start(out=ktc[b][:, 0:nb * 128], in_=ktscr[which][c][:, k0 * 128:k1 * 128]),
                    reads=[("kts", which, g) for g in gs], writes=[("ktc", b)], chan="ktc%d" % b)
                ADD("pool", lambda e: e.dma_start(out=vc[b][:, 0:nb * 128].rearrange("p (b f) -> p b f", f=128), in_=vscr[which][c][:, k0:k1, :]),
                    reads=[("vts", which, g) for g in gs], writes=[("vc", b)], chan="vc%d" % b)
                chbuf[ch] = b

            def cols_of(kb):
                return max(0, kb - kb0) * 128, T

            def emit_Z(n):
                kb = order[n]
                c0, c1 = cols_of(kb)
                b = chbuf[kb // 16]
                kl = kb % 16
                for hh in range(2):
                    hp = hh * 64
                    mm(psZ[:, hh * 512 + c0:hh * 512 + c1], ktc[b][hp:hp + 64, kl * 128:(kl + 1) * 128], QT[hp:hp + 64, c, c0:c1], True, True,
                       reads=[("ktc", b), ("QT", tp, c)], writes=[("ps", "Z")])

            def emit_expZ(n):
                kb = order[n]
                c0, c1 = cols_of(kb)
                e3 = n % 3
                ADD("act", lambda e: e.activation(out=tmpf[e3][:, :, c0:c1], in_=psZ3[:, :, c0:c1], func=AF.Exp),
                    reads=[("ps", "Z")], writes=[("tmpf", e3)])
                if kb >= kb0:
                    for hh in range(2):
                        ADD("dve", lambda e, hh=hh: e.tensor_tensor(out=tmpf[e3][:, hh, c0:c0 + 128], in0=tmpf[e3][:, hh, c0:c0 + 128], in1=maskM, op=ALU.mult),
                            reads=[("tmpf", e3), ("cst",)], writes=[("tmpf", e3)])

            def emit_ln(n):
                c0, c1 = cols_of(order[n])
                ei, e3 = n % 2, n % 3
                ADD("act", lambda e: e.activation(out=spb[ei][:, :, c0:c1], in_=tmpf[e3][:, :, c0:c1], func=AF.Ln, bias=1.0),
                    reads=[("tmpf", e3)], writes=[("spb", ei)])

            def emit_U(n):
                c0, c1 = cols_of(order[n])
                ei = n % 2
                for hh in range(2):
                    mm(psC[:, hh * 512 + c0:hh * 512 + c1], unegb, spb[ei][:, hh, c0:c1], n == 0, False,
                       reads=[("spb", ei), ("cstb",)], writes=[("ps", "C", hh)])

            def emit_expC(n):
                c0, c1 = cols_of(order[n])
                ei = n % 2
                for hh in range(2):
                    ADD("act", lambda e, hh=hh: e.activation(out=xxb[ei][:, hh, c0:c1], in_=psC[:, hh * 512 + c0:hh * 512 + c1], func=AF.Exp),
                        reads=[("ps", "C", hh)], writes=[("xxb", ei, hh)])

            def emit_L(n):
                c0, c1 = cols_of(order[n])
                if n == nst - 1:
                    return
                ei = n % 2
                for hh in range(2):
                    mm(psC[:, hh * 512 + c0:hh * 512 + c1], lnegb, spb[ei][:, hh, c0:c1], False, False,
                       reads=[("spb", ei), ("cstb",)], writes=[("ps", "C", hh)])

            def emit_w(n):
                c0, c1 = cols_of(order[n])
                ei, e3 = n % 2, n % 3
                ADD("pool", lambda e: e.tensor_tensor(out=wwb[ei][:, :, c0:c1], in0=tmpf[e3][:, :, c0:c1], in1=xxb[ei][:, :, c0:c1], op=ALU.mult),
                    reads=[("tmpf", e3), ("xxb", ei, 0), ("xxb", ei, 1)], writes=[("wwb", ei)])

            def emit_PV(n):
                kb = order[n]
                c0, c1 = cols_of(kb)
                b = chbuf[kb // 16]
                kl = kb % 16
                ei = n % 2
                for hh in range(2):
                    mm(PS["O"][hh * 64:(hh + 1) * 64, c0:c1], vc[b][:, kl * 128 + hh * 64:kl * 128 + (hh + 1) * 64], wwb[ei][:, hh, c0:c1],
                       n == 0, n == nst - 1, reads=[("vc", b), ("wwb", ei)], writes=[("ps", "O")])

            top = (nkb - 1) // 16
            load_chunk(top)
            if top >= 1:
                load_chunk(top - 1)
            emit_Z(0)
            emit_expZ(0)
            if nst > 1:
                emit_Z(1)
            for n in range(nst):
                kb = order[n]
                emit_ln(n)
                emit_U(n)
                if n + 1 < nst:
                    emit_expZ(n + 1)
                if n + 2 < nst:
                    emit_Z(n + 2)
                if n >= 1:
                    emit_PV(n - 1)
                if n >= 1 and kb % 16 == 15 and kb // 16 >= 1:
                    load_chunk(kb // 16 - 1)
                emit_expC(n)
                emit_L(n)
                emit_w(n)
                c0, c1 = cols_of(kb)
                yield (2 * (2 * (c1 - c0) + 224) + 2 * (c1 - c0 + 224)) / 1200.0
            emit_PV(nst - 1)
            ADD("dve", lambda e, c=c: e.tensor_copy(out=mixT[:, 4 + c, 0:T], in_=PS["O"][:, 0:T]),
                reads=[("ps", "O")], writes=[("mixT", tp, 4 + c)])
            yield 0.5

    def genA(t):
        which, tok0, T, ntok, kb0, tp = t["which"], t["tok0"], t["T"], t["ntok"], t["kb0"], t["par"]
        xd, nkd, nvd, ncd = t["xd"], t["nkd"], t["nvd"], t["ncd"]
        hT, mixT, QT = hTs[tp], mixTs[tp], QTs[tp]
        nq = T // 128
        wT = 0.22 * T / 512
        for tb in range(nq):
            r = min(128, ntok - tb * 128)
            b = ctx["xs"] % 2
            ctx["xs"] += 1
            ADD("sp", lambda e, b=b, tb=tb, r=r: e.dma_start(out=xs[b][0:r, :], in_=xd[tok0 + tb * 128:tok0 + tb * 128 + r, :]),
                writes=[("xs", b)], chan="xs%d" % b)
            for g in range(2):
                pn = "G" if g == 0 else "U"
                for q in range(4):
                    dc = g * 4 + q
                    ADD("pe", lambda e, pn=pn, q=q, b=b, dc=dc: e.transpose(PS[pn][:, q * 128:(q + 1) * 128], xs[b][:, dc * 128:(dc + 1) * 128], ident),
                        reads=[("xs", b), ("cst",)], writes=[("ps", pn)])
                ADD("dve", lambda e, pn=pn, g=g, tb=tb: e.tensor_copy(out=hT[:, g * 4:(g + 1) * 4, tb * 128:(tb + 1) * 128],
                                                                       in_=PS[pn][:].rearrange("p (a b) -> p a b", a=4)),
                    reads=[("ps", pn)], writes=[("hT", tp, g * 4 + q) for q in range(4)])
            yield 2.0
        rmsnorm(T, tp, 0)
        yield 8 * wT
        yield from ffn(T, tp, 0)
        sid = 19
        rmsnorm(T, tp, 8)
        yield 8 * wT
        b0, sl0 = get_slab(sid); sid += 1
        b1, sl1 = get_slab(sid); sid += 1
        if t["hist"] == "zero":
            ADD("pool", lambda e: e.memset(u[:, :, 0:30], 0.0), writes=[("u", cc) for cc in range(4)])
        elif t["hist"] == "state":
            ADD("sp", lambda e: e.dma_start(out=sc_sb[0:30, :], in_=sc[:, :]), writes=[("sc_sb",)], chan="sc")
            for cc in range(4):
                ADD("pe", lambda e, cc=cc: e.transpose(PS["X"][:, cc * 32:cc * 32 + 30], sc_sb[0:30, cc * 128:(cc + 1) * 128], ident[0:30, 0:30]),
                    reads=[("sc_sb",), ("cst",)], writes=[("ps", "X")])
            ADD("dve", lambda e: e.tensor_copy(out=u[:, :, 0:30], in_=PS["X"][:, 0:128].rearrange("p (a b) -> p a b", a=4)[:, :, 0:30]),
                reads=[("ps", "X")], writes=[("u", cc) for cc in range(4)])
        for cc in range(4):
            for kc in range(8):
                mm(PS["G"][:, 0:T], sl0[:, kc, cc * 128:(cc + 1) * 128], xn[:, kc, 0:T], kc == 0, kc == 7,
                   reads=[("wslab", b0), ("xn", kc)], writes=[("ps", "G")])
                if kc % 4 == 3:
                    yield 4 * 0.22 * T / 512
            for kc in range(8):
                mm(PS["U"][:, 0:T], sl1[:, kc, cc * 128:(cc + 1) * 128], xn[:, kc, 0:T], kc == 0, kc == 7,
                   reads=[("wslab", b1), ("xn", kc)], writes=[("ps", "U")])
                if kc % 4 == 3:
                    yield 4 * 0.22 * T / 512
            tb_ = cc % 2
            ADD("act", lambda e, tb_=tb_: e.activation(out=eet[tb_][:, 0:T], in_=PS["U"][:, 0:T], func=AF.Exp, scale=-1.0),
                reads=[("ps", "U")], writes=[("eet", tb_)])
            recip1p(tb_, T)
            ADD("dve", lambda e, tb_=tb_, cc=cc: e.tensor_tensor(out=u[:, cc, 30:30 + T], in0=eet[tb_][:, 0:T], in1=PS["G"][:, 0:T], op=ALU.mult),
                reads=[("eet", tb_), ("ps", "G")], writes=[("u", cc)])
            yield 16 * wT
        b2, sl2 = get_slab(sid); sid += 1
        for c in range(4):
            pa = "G" if c % 2 == 0 else "U"
            for kc in range(8):
                mm(PS[pa][:, 0:T], sl2[:, kc, c * 128:(c + 1) * 128], xn[:, kc, 0:T], kc == 0, kc == 7,
                   reads=[("wslab", b2), ("xn", kc)], writes=[("ps", pa)])
                if kc % 4 == 3:
                    yield 4 * 0.22 * T / 512
            ADD("dve", lambda e, pa=pa, c=c: e.tensor_scalar(out=QT[:, c, 0:T], in0=PS[pa][:, 0:T], scalar1=0.125, scalar2=None, op0=ALU.mult),
                reads=[("ps", pa)], writes=[("QT", tp, c)])
            yield 8 * wT
        b3, sl3 = get_slab(sid); sid += 1
        for c in range(4):
            pa = "G" if c % 2 == 0 else "U"
            for kc in range(8):
                mm(PS[pa][:, 0:T], sl3[:, kc, c * 128:(c + 1) * 128], xn[:, kc, 0:T], kc == 0, kc == 7,
                   reads=[("wslab", b3), ("xn", kc)], writes=[("ps", pa)])
                if kc % 4 == 3:
                    yield 4 * 0.22 * T / 512
            ADD("dve", lambda e, pa=pa, c=c: e.tensor_copy(out=KTt[:, c, 0:T], in_=PS[pa][:, 0:T]),
                reads=[("ps", pa)], writes=[("KTt",)])
            yield 8 * wT

        def tokmajor(bslab, sl, outd, also_bf):
            for tb in range(nq):
                r = min(128, ntok - tb * 128)
                ob = ctx["kvo"] % 2
                pX = ("X", "G", "U")[ctx["kvo"] % 3]
                ctx["kvo"] += 1
                for kc in range(8):
                    mm(PS[pX][:, :], xn[:, kc, tb * 128:(tb + 1) * 128], sl[:, kc, :], kc == 0, kc == 7,
                       reads=[("wslab", bslab), ("xn", kc)], writes=[("ps", pX)])
                    if kc % 4 == 3:
                        yield 4 * 0.22 * T / 512
                ADD("dve", lambda e, ob=ob, pX=pX: e.tensor_copy(out=kvo[ob][:], in_=PS[pX][:, :]),
                    reads=[("ps", pX)], writes=[("kvo", ob)])
                if also_bf:
                    ADD("dve", lambda e, ob=ob, tb=tb: e.tensor_copy(out=Vt[:, tb, :], in_=kvo[ob][:]),
                        reads=[("kvo", ob)], writes=[("Vt",)])
                if r > 0:
                    ADD("sp", lambda e, ob=ob, tb=tb, r=r: e.dma_start(out=outd[tok0 + tb * 128:tok0 + tb * 128 + r, :], in_=kvo[ob][0:r, :]),
                        reads=[("kvo", ob)], writes=[("outkv",)], chan="kvo%d" % ob)
                yield 8 * 0.22

        yield from tokmajor(b3, sl3, nkd, False)
        b4, sl4 = get_slab(sid); sid += 1
        yield from tokmajor(b4, sl4, nvd, True)
        kv_store(which, kb0 * 128, T)

        for cc in range(4):
            ADD("dve", lambda e, cc=cc: e.tensor_copy(out=ub[:, cc, 0:30 + T], in_=u[:, cc, 0:30 + T]),
                reads=[("u", cc)], writes=[("ub", cc)])
        for cc in range(4):
            pX = ("X", "G", "U")[cc % 3]
            for k in range(31):
                r = ctx["dg"] % 4
                ctx["dg"] += 1
                wc = 40 + cc * 31 + k
                ADD("dve", lambda e, r=r, wc=wc: e.tensor_scalar(out=dg[r][:], in0=identb, scalar1=par[:, wc:wc + 1], scalar2=None, op0=ALU.mult),
                    reads=[("cstb",), ("par",)], writes=[("dg", r)])
                mm(PS[pX][:, 0:T], dg[r][:], ub[:, cc, k:k + T], k == 0, k == 30,
                   reads=[("dg", r), ("ub", cc)], writes=[("ps", pX)])
                if k % 4 == 3:
                    yield 4 * wT
            ADD("dve", lambda e, cc=cc, pX=pX: e.tensor_scalar(out=yacc[:, cc, 0:T], in0=PS[pX][:, 0:T], scalar1=par[:, 164 + cc:165 + cc], scalar2=None, op0=ALU.add),
                reads=[("ps", pX), ("par",)], writes=[("act", 2 * cc), ("act", 2 * cc + 1)])
            yield 3 * wT
        if t["conv_cols"] is not None:
            cs = t["conv_cols"]
            for cc in range(4):
                ADD("pe", lambda e, cc=cc: e.transpose(PS["X"][0:30, cc * 128:(cc + 1) * 128], u[:, cc, cs:cs + 30], ident),
                    reads=[("u", cc), ("cst",)], writes=[("ps", "X")])
            ADD("dve", lambda e: e.tensor_copy(out=cvo[0:30, :], in_=PS["X"][0:30, :]), reads=[("ps", "X")], writes=[("cvo",)])
            ADD("sp", lambda e: e.dma_start(out=ncd[:, :], in_=cvo[0:30, :]), reads=[("cvo",)], writes=[("outc",)], chan="cvo")
        ADD("dve", lambda e: e.tensor_copy(out=u[:, :, 0:30], in_=u[:, :, T:T + 30]),
            reads=[("u", cc) for cc in range(4)], writes=[("u", cc) for cc in range(4)])
        for cc in range(4):
            ADD("dve", lambda e, cc=cc: e.tensor_copy(out=xn[:, cc, 0:T], in_=yacc[:, cc, 0:T]),
                reads=[("act", 2 * cc), ("act", 2 * cc + 1)], writes=[("xn", cc)])
            ADD("dve", lambda e, cc=cc: e.tensor_tensor(out=xn[:, 4 + cc, 0:T], in0=yacc[:, cc, 0:T], in1=yacc[:, cc, 0:T], op=ALU.mult),
                reads=[("act", 2 * cc), ("act", 2 * cc + 1)], writes=[("xn", 4 + cc)])
        for cc in range(4):
            mm(PS["G"][:, 0:T], onesb, xn[:, cc, 0:T], cc == 0, cc == 3, reads=[("xn", cc), ("cstb",)], writes=[("ps", "G")])
        for cc in range(4):
            mm(PS["U"][:, 0:T], onesb, xn[:, 4 + cc, 0:T], cc == 0, cc == 3, reads=[("xn", 4 + cc), ("cstb",)], writes=[("ps", "U")])
        ADD("dve", lambda e: e.tensor_scalar(out=lnm[:, 0:T], in0=PS["G"][:, 0:T], scalar1=1.0 / 512, scalar2=None, op0=ALU.mult),
            reads=[("ps", "G")], writes=[("lnm",)])
        ADD("dve", lambda e: e.tensor_tensor(out=lnv[:, 0:T], in0=lnm[:, 0:T], in1=lnm[:, 0:T], op=ALU.mult),
            reads=[("lnm",)], writes=[("lnv",)])
        ADD("dve", lambda e: e.scalar_tensor_tensor(out=lnv[:, 0:T], in0=PS["U"][:, 0:T], scalar=1.0 / 512, in1=lnv[:, 0:T], op0=ALU.mult, op1=ALU.subtract),
            reads=[("ps", "U"), ("lnv",)], writes=[("lnv",)])
        ADD("dve", lambda e: e.tensor_scalar(out=lnv[:, 0:T], in0=lnv[:, 0:T], scalar1=0.0, scalar2=None, op0=ALU.max),
            reads=[("lnv",)], writes=[("lnv",)])
        ADD("act", lambda e: e.activation(out=lnv[:, 0:T], in_=lnv[:, 0:T], func=AF.Ln, bias=EPS),
            reads=[("lnv",)], writes=[("lnv",)])
        ADD("act", lambda e: e.activation(out=lnv[:, 0:T], in_=lnv[:, 0:T], func=AF.Exp, scale=-0.5),
            reads=[("lnv",)], writes=[("lnv",)])
        yield 8 * wT
        for cc in range(4):
            eng = "dve"
            ADD(eng, lambda e, cc=cc: e.tensor_tensor(out=yacc[:, cc, 0:T], in0=yacc[:, cc, 0:T], in1=lnm[:, 0:T], op=ALU.subtract),
                reads=[("act", 2 * cc), ("act", 2 * cc + 1), ("lnm",)], writes=[("act", 2 * cc), ("act", 2 * cc + 1)])
            ADD(eng, lambda e, cc=cc: e.tensor_tensor(out=yacc[:, cc, 0:T], in0=yacc[:, cc, 0:T], in1=lnv[:, 0:T], op=ALU.mult),
                reads=[("act", 2 * cc), ("act", 2 * cc + 1), ("lnv",)], writes=[("act", 2 * cc), ("act", 2 * cc + 1)])
            ADD("dve", lambda e, cc=cc: e.tensor_scalar(out=yacc[:, cc, 0:T], in0=yacc[:, cc, 0:T], scalar1=par[:, 168 + cc:169 + cc],
                                                        scalar2=par[:, 172 + cc:173 + cc], op0=ALU.mult, op1=ALU.add),
                reads=[("act", 2 * cc), ("act", 2 * cc + 1), ("par",)], writes=[("act", 2 * cc), ("act", 2 * cc + 1)])
            tb_ = cc % 2
            ADD("act", lambda e, cc=cc, tb_=tb_: e.activation(out=eet[tb_][:, 0:T], in_=yacc[:, cc, 0:T], func=AF.Exp, scale=-1.0),
                reads=[("act", 2 * cc), ("act", 2 * cc + 1)], writes=[("eet", tb_)])
            recip1p(tb_, T)
            ADD("dve", lambda e, cc=cc, tb_=tb_: e.tensor_tensor(out=mixT[:, cc, 0:T], in0=yacc[:, cc, 0:T], in1=eet[tb_][:, 0:T], op=ALU.mult),
                reads=[("act", 2 * cc), ("act", 2 * cc + 1), ("eet", tb_)], writes=[("mixT", tp, cc)])
        yield 1.0

    def genC(t):
        tok0, T, ntok, tp = t["tok0"], t["T"], t["ntok"], t["par"]
        pd, yd = t["pd"], t["yd"]
        hT, mixT = hTs[tp], mixTs[tp]
        nq = T // 128
        wT = 0.22 * T / 512
        sid = 24
        for tb in range(nq):
            r = min(128, ntok - tb * 128)
            b = ctx["ps"] % 2
            ctx["ps"] += 1
            ADD("sp", lambda e, b=b, tb=tb, r=r: e.dma_start(out=pstg[b][0:r, :], in_=pd[tok0 + tb * 128:tok0 + tb * 128 + r, :]),
                writes=[("pstg", b)], chan="pstg%d" % b)
            for kc in range(2):
                ADD("pe", lambda e, b=b, kc=kc: e.transpose(PS["G"][:, kc * 128:(kc + 1) * 128], pstg[b][:, kc * 128:(kc + 1) * 128], ident),
                    reads=[("pstg", b), ("cst",)], writes=[("ps", "G")])
            ADD("dve", lambda e, tb=tb: e.tensor_copy(out=pT[:, :, tb * 128:(tb + 1) * 128], in_=PS["G"][:, 0:256].rearrange("p (a b) -> p a b", a=2)),
                reads=[("ps", "G")], writes=[("pT",)])
        bo = []
        for o in range(2):
            bo.append(get_slab(sid)); sid += 1
        for oc in range(8):
            b, sl = bo[oc // 4]
            pX = ("X", "G", "U")[oc % 3]
            for kc in range(8):
                mm(PS[pX][:, 0:T], sl[:, kc, (oc % 4) * 128:(oc % 4 + 1) * 128], mixT[:, kc, 0:T], kc == 0, kc == 7,
                   reads=[("wslab", b), ("mixT", tp, kc)], writes=[("ps", pX)])
                if kc % 4 == 3:
                    yield 4 * 0.22 * T / 512
            ADD("dve", lambda e, oc=oc, pX=pX: e.tensor_tensor(out=hT[:, oc, 0:T], in0=hT[:, oc, 0:T], in1=PS[pX][:, 0:T], op=ALU.add),
                reads=[("ps", pX), ("hT", tp, oc)], writes=[("hT", tp, oc)])
            yield 8 * wT
        rmsnorm(T, tp, 16)
        yield 8 * wT
        yield from ffn(T, tp, sid)
        sid += 19
        rmsnorm(T, tp, 24)
        yield 8 * wT
        bg = []
        for o in range(2):
            bg.append(get_slab(sid)); sid += 1
        bw, slw = get_slab(sid); sid += 1
        for oc in range(8):
            b, sl = bg[oc // 4]
            for kc in range(8):
                mm(PS["G"][:, 0:T], sl[:, kc, (oc % 4) * 128:(oc % 4 + 1) * 128], xn[:, kc, 0:T], kc == 0, kc == 7,
                   reads=[("wslab", b), ("xn", kc)], writes=[("ps", "G")])
                if kc % 4 == 3:
                    yield 4 * 0.22 * T / 512
            for kc in range(2):
                mm(PS["U"][:, 0:T], slw[:, kc, oc * 128:(oc + 1) * 128], pT[:, kc, 0:T], kc == 0, kc == 1,
                   reads=[("wslab", bw), ("pT",)], writes=[("ps", "U")])
            tb_ = oc % 2
            ADD("act", lambda e, tb_=tb_: e.activation(out=eet[tb_][:, 0:T], in_=PS["G"][:, 0:T], func=AF.Exp, scale=-1.0),
                reads=[("ps", "G")], writes=[("eet", tb_)])
            recip1p(tb_, T)
            ADD("dve", lambda e, tb_=tb_: e.tensor_tensor(out=sgt[tb_][:, 0:T], in0=eet[tb_][:, 0:T], in1=PS["U"][:, 0:T], op=ALU.mult),
                reads=[("eet", tb_), ("ps", "U")], writes=[("sgt", tb_)])
            ADD("dve", lambda e, tb_=tb_, oc=oc: e.tensor_tensor(out=hT[:, oc, 0:T], in0=hT[:, oc, 0:T], in1=sgt[tb_][:, 0:T], op=ALU.add),
                reads=[("sgt", tb_), ("hT", tp, oc)], writes=[("hT", tp, oc)])
            yield 10 * wT
        rmsnorm(T, tp, 32, final=True)
        yield 8 * wT
        for tb in range(nq):
            r = min(128, ntok - tb * 128)
            if r <= 0:
                continue
            b = ctx["ys"] % 2
            ctx["ys"] += 1
            for g in range(2):
                pn = "G" if g == 0 else "U"
                for q in range(4):
                    dc = g * 4 + q
                    ADD("pe", lambda e, pn=pn, q=q, dc=dc, tb=tb: e.transpose(PS[pn][:, q * 128:(q + 1) * 128], hT[:, dc, tb * 128:(tb + 1) * 128], ident),
                        reads=[("hT", tp, dc), ("cst",)], writes=[("ps", pn)])
                ADD("dve", lambda e, pn=pn, b=b, g=g: e.tensor_copy(out=ys[b][:, g * 512:(g + 1) * 512], in_=PS[pn][:, :]),
                    reads=[("ps", pn)], writes=[("ys", b)])
            ADD("sp", lambda e, b=b, tb=tb, r=r: e.dma_start(out=yd[tok0 + tb * 128:tok0 + tb * 128 + r, :], in_=ys[b][0:r, :]),
                reads=[("ys", b)], writes=[("outy",)], chan="ys%d" % b)
            yield 2.0

    tiles = []
    if do_sample:
        for grp in range(4):
            for tb in range(4):
                blk = grp * 4 + tb
                b = ctx["xs"] % 2
                ctx["xs"] += 1
                ADD("sp", lambda e, b=b, blk=blk: e.dma_start(out=xs[b][:, 0:512], in_=ck[blk * 128:(blk + 1) * 128, :]),
                    writes=[("xs", b)], chan="xs%d" % b)
                for c in range(4):
                    ADD("pe", lambda e, b=b, c=c: e.transpose(PS["O"][:, c * 128:(c + 1) * 128], xs[b][:, c * 128:(c + 1) * 128], ident),
                        reads=[("xs", b), ("cst",)], writes=[("ps", "O")])
                ADD("dve", lambda e, tb=tb: e.tensor_copy(out=KTt[:, :, tb * 128:(tb + 1) * 128], in_=PS["O"][:].rearrange("p (a b) -> p a b", a=4)),
                    reads=[("ps", "O")], writes=[("KTt",)])
                ADD("sp", lambda e, b=b, blk=blk: e.dma_start(out=xs[b][:, 512:1024], in_=cv[blk * 128:(blk + 1) * 128, :]),
                    writes=[("xs", b)], chan="xv%d" % b)
                ADD("pool", lambda e, b=b, tb=tb: e.tensor_copy(out=Vt[:, tb, :], in_=xs[b][:, 512:1024]),
                    reads=[("xs", b)], writes=[("Vt",)])
            kv_store("s", grp * 512, 512)
        for b in range(2):
            ADD("pool", lambda e, b=b: e.memset(xs[b][:], 0.0), writes=[("xs", b)])
        tiles.append(dict(which="s", tok0=0, T=128, ntok=TS, kb0=16, xd=x_s, pd=p_s, yd=y_s, nkd=nk_s, nvd=nv_s, ncd=nc_s,
                          hist="state", conv_cols=30 + TS - 30))
    for i in range(n_ptiles):
        last = (i == n_ptiles - 1)
        tiles.append(dict(which="p", tok0=i * 512, T=512, ntok=512, kb0=4 * i, xd=x_p, pd=p_p, yd=y_p, nkd=nk_p, nvd=nv_p, ncd=nc_p,
                          hist="zero" if i == 0 else "carry", conv_cols=(30 + 512 - 30) if last else None))
    for i, t in enumerate(tiles):
        t["par"] = i % 2

    def run_all(g):
        for _ in g:
            pass

    def total_weight(mk):
        saved = dict(ctx)
        real = PH[0]
        PH[0] = DummyProg()
        tot = sum(w for w in mk())
        PH[0] = real
        ctx.clear(); ctx.update(saved)
        return tot

    def chain(*gens):
        for g in gens:
            yield from g

    def interleave(mkA, mkD):
        ta, td = max(total_weight(mkA), 1e-9), max(total_weight(mkD), 1e-9)
        ga, gd = mkA(), mkD()
        da = dd = 0.0
        a_alive = d_alive = True
        while a_alive or d_alive:
            pick_a = a_alive and (not d_alive or da / ta - 0.15 <= dd / td)
            if pick_a:
                try:
                    da += next(ga)
                except StopIteration:
                    a_alive = False
            else:
                try:
                    dd += next(gd)
                except StopIteration:
                    d_alive = False

    NTL = len(tiles)
    if not pipelined:
        for t in tiles:
            run_all(genA(t)); run_all(genB(t)); run_all(genC(t))
    else:
        run_all(genA(tiles[0]))
        for i in range(NTL):
            ctx["recip_act"] = (tiles[i]["kb0"] + tiles[i]["T"] // 128) <= RECIP_ACT_MAX_KB

            def mkD(i=i):
                gens = []
                if i >= 1:
                    gens.append(genC(tiles[i - 1]))
                if i + 1 < NTL:
                    gens.append(genA(tiles[i + 1]))
                return chain(*gens)
            interleave(lambda i=i: genB(tiles[i]), mkD)
        ctx["recip_act"] = True
        run_all(genC(tiles[NTL - 1]))

    P = PH[0]
    P.emit(nc, stack)
    stack.close()
    return nc, P


_CACHE = {}


def _consts():
    c = np.zeros((128, 512), np.float32)
    s = np.arange(128)[:, None]
    j = np.arange(128)[None, :]
    c[:, 0:128] = np.eye(128, dtype=np.float32)
    c[:, 128:256] = -(s >= j).astype(np.float32)
    c[:, 256:384] = -(s < j).astype(np.float32)
    c[:, 384:512] = (s < j).astype(np.float32)
    return c


def _params(inp):
    p = np.zeros((128, NPAR), np.float32)

    def col(v, n):
        return np.ascontiguousarray(np.asarray(v, np.float32).reshape(n, 128).T)

    p[:, 0:8] = col(inp["ffn1_norm"][0], 8)
    p[:, 8:16] = col(inp["mix_norm"][0], 8)
    p[:, 16:24] = col(inp["ffn2_norm"][0], 8)
    p[:, 24:32] = col(inp["ple_norm"][0], 8)
    p[:, 32:40] = col(inp["final_norm"], 8)
    cw = np.asarray(inp["conv_w"][0], np.float32)
    p[:, 40:164] = cw.T.reshape(4, 128, 31).transpose(1, 0, 2).reshape(128, 124)
    p[:, 164:168] = col(inp["conv_b"][0], 4)
    p[:, 168:172] = col(inp["conv_ln_g"][0], 4)
    p[:, 172:176] = col(inp["conv_ln_b"][0], 4)
    return p


def kernel(**inp):
    inp = {k: np.asarray(v) for k, v in inp.items()}
    if "nc" not in _CACHE:
        _CACHE["nc"] = build_nc()[0]
    nc = _CACHE["nc"]
    consts = _consts()
    params = _params(inp)
    shared = {
        "f1gu": np.ascontiguousarray(inp["ffn1_w_gu"][0]), "f1dn": np.ascontiguousarray(inp["ffn1_w_down"][0]),
        "win": np.ascontiguousarray(inp["w_in"][0]), "wout": np.ascontiguousarray(inp["w_out"][0]),
        "f2gu": np.ascontiguousarray(inp["ffn2_w_gu"][0]), "f2dn": np.ascontiguousarray(inp["ffn2_w_down"][0]),
        "pgate": np.ascontiguousarray(inp["ple_gate_w"][0]), "plew": np.ascontiguousarray(inp["ple_w"][0]),
        "params": params, "consts": consts,
    }
    in_maps = []
    for c in range(8):
        m = dict(shared)
        m["x_p"] = np.ascontiguousarray(inp["x_prompt"][c])
        m["x_s"] = np.ascontiguousarray(inp["x_sample"][c])
        m["p_p"] = np.ascontiguousarray(inp["p_prompt"][0, c])
        m["p_s"] = np.ascontiguousarray(inp["p_sample"][0, c])
        m["ck"] = np.ascontiguousarray(inp["cache_k"][0, c].reshape(PAST, 512))
        m["cv"] = np.ascontiguousarray(inp["cache_v"][0, c].reshape(PAST, 512))
        m["sc"] = np.ascontiguousarray(inp["state_conv"][0, c])
        in_maps.append(m)
    res = run_bass_kernel_spmd(nc, in_maps, core_ids=list(range(8)))
    R = res.results

    def g(name):
        return np.stack([np.asarray(R[c][name], np.float32) for c in range(8)])

    y_prompt = g("y_p")
    y_sample = g("y_s")
    nk_p = g("nk_p").reshape(1, 8, SEQ, 8, 64)
    nv_p = g("nv_p").reshape(1, 8, SEQ, 8, 64)
    nc_p = g("nc_p").reshape(1, 8, 30, 512)
    nk_s = g("nk_s").reshape(1, 8, TS, 8, 64)
    nv_s = g("nv_s").reshape(1, 8, TS, 8, 64)
    nc_s = g("nc_s").reshape(1, 8, 30, 512)
    return (y_prompt, y_sample, nk_p, nv_p, nc_p, nk_s, nv_s, nc_s)
```

```python
import numpy as np
from contextlib import ExitStack
import concourse.bass as bass
import concourse.mybir as mybir
from concourse.bass_utils import run_bass_kernel_spmd

F32 = mybir.dt.float32
BF16 = mybir.dt.bfloat16
ALU = mybir.AluOpType
AF = mybir.ActivationFunctionType

D = 1024
DFF = 2816
SEQ = 8192
TS = 64
PAST = 2048
EPS = 1e-6
NSLAB = 48
RECIP_ACT_MAX_KB = 9999
NPAR = 176
ENGS = ("pe", "act", "dve", "pool", "sp")


class Ins:
    __slots__ = ("eng", "fn", "deps", "signal", "val", "chan")


class Prog:
    def __init__(self):
        self.streams = {e: [] for e in ENGS}
        self.lastw = {}
        self.readers = {}
        self.chan_count = {}

    def _need(self, p, c, raw):
        if p.chan is not None:
            return True
        if c.chan is None and p.eng == c.eng:
            if p.eng == "pool":
                return True
            return raw and p.eng != "pe"
        return True

    def add(self, eng, fn, reads=(), writes=(), chan=None):
        c = Ins()
        c.eng, c.fn, c.chan, c.signal, c.val = eng, fn, chan, chan is not None, 0
        deps = {}
        for r in reads:
            p = self.lastw.get(r)
            if p is not None and self._need(p, c, True):
                deps[id(p)] = p
        for w in writes:
            p = self.lastw.get(w)
            if p is not None and self._need(p, c, False):
                deps[id(p)] = p
            for p in self.readers.get(w, {}).values():
                if self._need(p, c, False):
                    deps[id(p)] = p
        c.deps = list(deps.values())
        for p in c.deps:
            p.signal = True
        for w in writes:
            self.lastw[w] = c
            self.readers[w] = {}
        for r in reads:
            if r not in writes:
                self.readers.setdefault(r, {})[("ch", chan) if chan is not None else eng] = c
        self.streams[eng].append(c)
        return c

    def emit(self, nc, stack):
        chans = set()
        for e in ENGS:
            for c in self.streams[e]:
                if c.chan is not None:
                    chans.add(c.chan)
        sem = {e: stack.enter_context(nc.semaphore("s_" + e)) for e in ENGS}
        for ch in sorted(chans):
            sem[("ch", ch)] = stack.enter_context(nc.semaphore("c_" + ch))
        cnt = {}
        for e in ENGS:
            for c in self.streams[e]:
                if c.chan is not None:
                    k = ("ch", c.chan)
                    cnt[k] = cnt.get(k, 0) + 16
                    c.val = cnt[k]
                elif c.signal:
                    cnt[e] = cnt.get(e, 0) + 1
                    c.val = cnt[e]
        self.maxvals = dict(cnt)
        final = {k: v for k, v in cnt.items() if isinstance(k, tuple)}

        def run(e, handle):
            seen = {}
            for c in self.streams[e]:
                for p in c.deps:
                    k = ("ch", p.chan) if p.chan is not None else p.eng
                    if seen.get(k, 0) < p.val:
                        handle.wait_ge(sem[k], p.val)
                        seen[k] = p.val
                ins = c.fn(handle)
                if c.chan is not None:
                    ins.then_inc(sem[("ch", c.chan)], 16)
                elif c.signal:
                    ins.then_inc(sem[e], 1)
            if e == "sp":
                for k, v in final.items():
                    if seen.get(k, 0) < v:
                        handle.wait_ge(sem[k], v)

        with nc.Block() as block:
            @block.tensor
            def _(h):
                run("pe", h)

            @block.scalar
            def _(h):
                run("act", h)

            @block.vector
            def _(h):
                run("dve", h)

            @block.gpsimd
            def _(h):
                run("pool", h)

            @block.sync
            def _(h):
                run("sp", h)


def slab_defs():
    defs = []

    def ffn(pfx):
        for j in range(6):
            defs.append((pfx + "gu", 8, 512, [(0, 256, 0, 256 * j), (256, 256, 0, DFF + 256 * j)]))
        for ob in range(4):
            defs.append((pfx + "dn", 12, 256, [(0, 256, 0, 256 * ob)]))
        for j in range(6, 11):
            defs.append((pfx + "gu", 8, 512, [(0, 256, 0, 256 * j), (256, 256, 0, DFF + 256 * j)]))
        for ob in range(4):
            defs.append((pfx + "dn", 10, 256, [(0, 256, 12, 256 * ob)]))

    ffn("f1")
    for c in range(5):
        defs.append(("win", 8, 512, [(0, 512, 0, 512 * c)]))
    for o in range(2):
        defs.append(("wout", 8, 512, [(0, 512, 0, 512 * o)]))
    ffn("f2")
    for o in range(2):
        defs.append(("pgate", 8, 512, [(0, 512, 0, 512 * o)]))
    defs.append(("plew", 2, 1024, [(0, 1024, 0, 0)]))
    assert len(defs) == NSLAB
    return defs


class DummyProg:
    def add(self, *a, **k):
        return None


def build_nc(n_ptiles=16, do_sample=True, pipelined=True):
    nc = bass.Bass("TRN2", target_bir_lowering=False)
    PH = [Prog()]
    stack = ExitStack()

    def ADD(*a, **k):
        return PH[0].add(*a, **k)

    def din(name, shape):
        return nc.dram_tensor(name, list(shape), F32, kind="ExternalInput").ap()

    def dout(name, shape):
        return nc.dram_tensor(name, list(shape), F32, kind="ExternalOutput").ap()

    x_p = din("x_p", [SEQ, D]); x_s = din("x_s", [TS, D])
    p_p = din("p_p", [SEQ, 256]); p_s = din("p_s", [TS, 256])
    ck = din("ck", [PAST, 512]); cv = din("cv", [PAST, 512]); sc = din("sc", [30, 512])
    W = {
        "f1gu": din("f1gu", [D, 2 * DFF]), "f1dn": din("f1dn", [DFF, D]),
        "win": din("win", [D, 2560]), "wout": din("wout", [D, D]),
        "f2gu": din("f2gu", [D, 2 * DFF]), "f2dn": din("f2dn", [DFF, D]),
        "pgate": din("pgate", [D, D]), "plew": din("plew", [256, D]),
    }
    params_d = din("params", [128, NPAR])
    consts_d = din("consts", [128, 512])
    y_p = dout("y_p", [SEQ, D]); y_s = dout("y_s", [TS, D])
    nk_p = dout("nk_p", [SEQ, 512]); nv_p = dout("nv_p", [SEQ, 512]); nc_p = dout("nc_p", [30, 512])
    nk_s = dout("nk_s", [TS, 512]); nv_s = dout("nv_s", [TS, 512]); nc_s = dout("nc_s", [30, 512])

    wsc = nc.dram_tensor("wsc", [NSLAB, 128, 4096], BF16).ap()
    ktscr = {"p": nc.dram_tensor("ktscr_p", [4, 128, SEQ], BF16).ap(),
             "s": nc.dram_tensor("ktscr_s", [4, 128, PAST + 128], BF16).ap()}
    vscr = {"p": nc.dram_tensor("vscr_p", [4, 128, 64, 128], BF16).ap(),
            "s": nc.dram_tensor("vscr_s", [4, 128, 17, 128], BF16).ap()}

    def sb(name, shape, dt):
        return stack.enter_context(nc.sbuf_tensor(name, list(shape), dt))

    def pst(name):
        return stack.enter_context(nc.psum_tensor(name, [128, 512], F32))

    ktc = [sb("ktc%d" % i, [128, 2048], BF16) for i in range(2)]
    vc = [sb("vc%d" % i, [128, 2048], BF16) for i in range(2)]
    hTs = [sb("hT%d" % i, [128, 8, 512], F32) for i in range(2)]
    mixTs = [sb("mixT%d" % i, [128, 8, 512], BF16) for i in range(2)]
    QTs = [sb("QT%d" % i, [128, 4, 512], BF16) for i in range(2)]
    xn = sb("xn", [128, 8, 512], BF16)
    act = sb("act", [128, 12, 512], BF16)
    KTt = sb("KTt", [128, 4, 512], BF16)
    Vt = sb("Vt", [128, 4, 512], BF16)
    u = sb("u", [128, 4, 542], F32)
    ub = sb("ub", [128, 4, 542], BF16)
    dg = [sb("dg%d" % i, [128, 128], BF16) for i in range(4)]
    wslab = [sb("wslab%d" % i, [128, 4096], BF16) for i in range(3)]
    tmpf = [sb("tmpf%d" % i, [128, 2, 512], F32) for i in range(3)]
    spb = [sb("spb%d" % i, [128, 2, 512], BF16) for i in range(2)]
    xxb = [sb("xxb%d" % i, [128, 2, 512], BF16) for i in range(2)]
    wwb = [sb("wwb%d" % i, [128, 2, 512], BF16) for i in range(2)]
    sgt = [sb("sgt%d" % i, [128, 512], F32) for i in range(2)]
    eet = [sb("eet%d" % i, [128, 512], F32) for i in range(2)]
    lnm = sb("lnm", [128, 512], F32)
    lnv = sb("lnv", [128, 512], F32)
    rstd = sb("rstd", [128, 512], F32)
    xs = [sb("xs%d" % i, [128, 1024], F32) for i in range(2)]
    ys = [sb("ys%d" % i, [128, 1024], F32) for i in range(2)]
    pstg = [sb("pstg%d" % i, [128, 256], F32) for i in range(2)]
    pT = sb("pT", [128, 2, 512], BF16)
    kvo = [sb("kvo%d" % i, [128, 512], F32) for i in range(2)]
    cst = sb("cst", [128, 512], F32)
    cstb = sb("cstb", [128, 512], BF16)
    par = sb("par", [128, NPAR], F32)
    sc_sb = sb("sc_sb", [32, 512], F32)
    cvo = sb("cvo", [32, 512], F32)

    psZ = stack.enter_context(nc.psum_tensor("psZ", [128, 1024], F32))
    psC = stack.enter_context(nc.psum_tensor("psC", [128, 1024], F32))
    PS = {n: pst("ps" + n) for n in ("O", "G", "U", "X")}
    psZ3 = psZ[:].rearrange("p (h q) -> p h q", h=2)
    psC3 = psC[:].rearrange("p (h q) -> p h q", h=2)
    ident = cst[:, 0:128]
    maskM = cst[:, 384:512]
    onesb = cstb[:, 0:128]
    unegb = cstb[:, 128:256]
    lnegb = cstb[:, 256:384]
    identb = cstb[:, 384:512]
    yacc = act[:, 0:8, :].rearrange("p a b -> p (a b)").bitcast(F32).rearrange("p (a b) -> p a b", a=4)
    ctx = {"slab": 0, "xs": 0, "ys": 0, "ps": 0, "kvo": 0, "chunk": 0, "recip_act": True, "dg": 0}

    ADD("sp", lambda e: e.dma_start(out=cst[:], in_=consts_d[:, :]), writes=[("cst",)], chan="cst")
    ADD("sp", lambda e: e.dma_start(out=par[:], in_=params_d[:, :]), writes=[("par",)], chan="par")
    ADD("dve", lambda e: e.tensor_copy(out=cstb[:, 128:384], in_=cst[:, 128:384]), reads=[("cst",)], writes=[("cstb",)])
    ADD("dve", lambda e: e.tensor_copy(out=cstb[:, 384:512], in_=cst[:, 0:128]), reads=[("cst",)], writes=[("cstb",)])
    ADD("dve", lambda e: e.memset(cstb[:, 0:128], 1.0), writes=[("cstb",)])

    def recip1p(ei, T):
        if ctx["recip_act"]:
            ADD("act", lambda e: e.activation(out=eet[ei][:, 0:T], in_=eet[ei][:, 0:T], func=AF.Ln, bias=1.0),
                reads=[("eet", ei)], writes=[("eet", ei)])
            ADD("act", lambda e: e.activation(out=eet[ei][:, 0:T], in_=eet[ei][:, 0:T], func=AF.Exp, scale=-1.0),
                reads=[("eet", ei)], writes=[("eet", ei)])
        else:
            ADD("dve", lambda e: e.tensor_scalar(out=eet[ei][:, 0:T], in0=eet[ei][:, 0:T], scalar1=1.0, scalar2=None, op0=ALU.add),
                reads=[("eet", ei)], writes=[("eet", ei)])
            ADD("dve", lambda e: e.reciprocal(out=eet[ei][:, 0:T], in_=eet[ei][:, 0:T]),
                reads=[("eet", ei)], writes=[("eet", ei)])
    for b in range(2):
        ADD("pool", lambda e, b=b: e.memset(xs[b][:], 0.0), writes=[("xs", b)])
        ADD("pool", lambda e, b=b: e.memset(pstg[b][:], 0.0), writes=[("pstg", b)])

    defs = slab_defs()
    for sid, (wn, KC, Wd, pieces) in enumerate(defs):
        b = sid % 2
        stg = hTs[b][:].rearrange("p a b -> p (a b)")
        stb = mixTs[b][:].rearrange("p a b -> p (a b)")
        wv = W[wn].rearrange("(k p) n -> p k n", p=128)
        n = KC * Wd
        dst3 = stg[:, 0:n].rearrange("p (k w) -> p k w", k=KC)
        for pi, (wo, wl, k0, c0) in enumerate(pieces):
            ADD("sp", lambda e, dst3=dst3, wv=wv, wo=wo, wl=wl, k0=k0, c0=c0, KC=KC:
                e.dma_start(out=dst3[:, :, wo:wo + wl], in_=wv[:, k0:k0 + KC, c0:c0 + wl]),
                writes=[("hT", b, "pre", pi)], chan="pre_l%d_%d" % (b, pi))
        eng = "dve" if sid % 2 == 0 else "pool"
        ADD(eng, lambda e, stb=stb, stg=stg, n=n: e.tensor_copy(out=stb[:, 0:n], in_=stg[:, 0:n]),
            reads=[("hT", b, "pre", 0), ("hT", b, "pre", 1)], writes=[("mixT", b, "pre")])
        ADD("pool", lambda e, stb=stb, sid=sid, n=n: e.dma_start(out=wsc[sid][:, 0:n], in_=stb[:, 0:n]),
            reads=[("mixT", b, "pre")], writes=[("wsc", sid)], chan="pre_s%d" % b)
    for b in range(2):
        ADD("pool", lambda e, b=b: e.memset(mixTs[b][:, 0, 0:8], 0.0),
            reads=[("hT", b, "pre", 0), ("hT", b, "pre", 1), ("mixT", b, "pre")],
            writes=[("hT", b, dc) for dc in range(8)] + [("mixT", b, dc) for dc in range(8)] + [("mixT", b, "pre")])

    def get_slab(sid):
        b = ctx["slab"] % 3
        ctx["slab"] += 1
        KC, Wd = defs[sid][1], defs[sid][2]
        n = KC * Wd
        ADD("sp", lambda e: e.dma_start(out=wslab[b][:, 0:n], in_=wsc[sid][:, 0:n]),
            reads=[("wsc", sid)], writes=[("wslab", b)], chan="wslab%d" % b)
        return b, wslab[b][:, 0:n].rearrange("p (k w) -> p k w", k=KC)

    def mm(out, lhsT, rhs, start, stop, reads, writes):
        ADD("pe", lambda e: e.matmul(out, lhsT, rhs, start=start, stop=stop, skip_group_check=True),
            reads=reads, writes=writes)

    def rmsnorm(T, hp, gcol0, final=False):
        hT = hTs[hp]
        for dc in range(8):
            eng = "dve"
            ADD(eng, lambda e, dc=dc: e.tensor_tensor(out=act[:, dc, 0:T], in0=hT[:, dc, 0:T], in1=hT[:, dc, 0:T], op=ALU.mult),
                reads=[("hT", hp, dc)], writes=[("act", dc)])
        for dc in range(8):
            mm(PS["X"][:, 0:T], onesb, act[:, dc, 0:T], dc == 0, dc == 7,
               reads=[("act", dc), ("cstb",)], writes=[("ps", "X")])
        ADD("act", lambda e: e.activation(out=rstd[:, 0:T], in_=PS["X"][:, 0:T], func=AF.Ln, bias=EPS, scale=1.0 / D),
            reads=[("ps", "X")], writes=[("rstd",)])
        ADD("act", lambda e: e.activation(out=rstd[:, 0:T], in_=rstd[:, 0:T], func=AF.Exp, scale=-0.5),
            reads=[("rstd",)], writes=[("rstd",)])
        for dc in range(8):
            if final:
                ADD("dve", lambda e, dc=dc: e.scalar_tensor_tensor(out=hT[:, dc, 0:T], in0=hT[:, dc, 0:T], scalar=par[:, gcol0 + dc:gcol0 + dc + 1],
                                                                    in1=rstd[:, 0:T], op0=ALU.mult, op1=ALU.mult),
                    reads=[("hT", hp, dc), ("rstd",), ("par",)], writes=[("hT", hp, dc)])
            else:
                ADD("dve", lambda e, dc=dc: e.scalar_tensor_tensor(out=xn[:, dc, 0:T], in0=hT[:, dc, 0:T], scalar=par[:, gcol0 + dc:gcol0 + dc + 1],
                                                                    in1=rstd[:, 0:T], op0=ALU.mult, op1=ALU.mult),
                    reads=[("hT", hp, dc), ("rstd",), ("par",)], writes=[("xn", dc)])

    def ffn(T, hp, sid0):
        hT = hTs[hp]
        sid = sid0
        fcg = 0
        w16 = 16 * 0.22 * T / 512
        for half, (ngu, nfc) in enumerate(((6, 12), (5, 10))):
            for j in range(ngu):
                b, sl = get_slab(sid); sid += 1
                for sub in range(2):
                    fcl = 2 * j + sub
                    for kc in range(8):
                        mm(PS["G"][:, 0:T], sl[:, kc, sub * 128:(sub + 1) * 128], xn[:, kc, 0:T], kc == 0, kc == 7,
                           reads=[("wslab", b), ("xn", kc)], writes=[("ps", "G")])
                        if kc % 4 == 3:
                            yield 4 * 0.22 * T / 512
                    tb = fcg % 2
                    ADD("act", lambda e, tb=tb: e.activation(out=eet[tb][:, 0:T], in_=PS["G"][:, 0:T], func=AF.Exp, scale=-1.0),
                        reads=[("ps", "G")], writes=[("eet", tb), ("gread", tb)])
                    ADD("dve", lambda e, tb=tb: e.tensor_copy(out=sgt[tb][:, 0:T], in_=PS["G"][:, 0:T]),
                        reads=[("ps", "G"), ("gread", tb)], writes=[("sgt", tb)])
                    for kc in range(8):
                        mm(PS["U"][:, 0:T], sl[:, kc, 256 + sub * 128:256 + (sub + 1) * 128], xn[:, kc, 0:T], kc == 0, kc == 7,
                           reads=[("wslab", b), ("xn", kc)], writes=[("ps", "U")])
                        if kc % 4 == 3:
                            yield 4 * 0.22 * T / 512
                    ADD("dve", lambda e, tb=tb: e.tensor_tensor(out=sgt[tb][:, 0:T], in0=sgt[tb][:, 0:T], in1=PS["U"][:, 0:T], op=ALU.mult),
                        reads=[("sgt", tb), ("ps", "U")], writes=[("sgt", tb)])
                    recip1p(tb, T)
                    ADD("dve", lambda e, tb=tb, fcl=fcl: e.tensor_tensor(out=act[:, fcl, 0:T], in0=sgt[tb][:, 0:T], in1=eet[tb][:, 0:T], op=ALU.mult),
                        reads=[("sgt", tb), ("eet", tb)], writes=[("act", fcl)])
                    fcg += 1
                    yield w16
            for ob in range(4):
                b, sl = get_slab(sid); sid += 1
                for o2 in range(2):
                    oc = 2 * ob + o2
                    pX = ("X", "G", "U")[oc % 3]
                    for fc in range(nfc):
                        mm(PS[pX][:, 0:T], sl[:, fc, o2 * 128:(o2 + 1) * 128], act[:, fc, 0:T], fc == 0, fc == nfc - 1,
                           reads=[("wslab", b), ("act", fc)], writes=[("ps", pX)])
                        if fc % 4 == 3:
                            yield 4 * 0.22 * T / 512
                    ADD("dve", lambda e, oc=oc, pX=pX: e.scalar_tensor_tensor(out=hT[:, oc, 0:T], in0=PS[pX][:, 0:T], scalar=0.5, in1=hT[:, oc, 0:T],
                                                                              op0=ALU.mult, op1=ALU.add),
                        reads=[("ps", pX), ("hT", hp, oc)], writes=[("hT", hp, oc)])
                    yield nfc * 0.22 * T / 512
        assert sid == sid0 + 19

    def kv_store(which, key0, T):
        nq = T // 128
        g = key0 // 512
        kdst = ktscr[which].rearrange("c p k -> p c k")
        ADD("sp", lambda e: e.dma_start(out=kdst[:, :, key0:key0 + T], in_=KTt[:, :, 0:T]),
            reads=[("KTt",)], writes=[("kts", which, g)], chan="kts")
        b0 = key0 // 128
        for c in range(4):
            ADD("sp", lambda e, c=c: e.dma_start(out=vscr[which][c][:, b0:b0 + nq, :], in_=Vt[:, 0:nq, c * 128:(c + 1) * 128]),
                reads=[("Vt",)], writes=[("vts", which, g)], chan="vts")

    def genB(t):
        which, T, kb0, tp = t["which"], t["T"], t["kb0"], t["par"]
        QT, mixT = QTs[tp], mixTs[tp]
        nq = T // 128
        nkb = kb0 + nq
        ZB = ("Z0", "Z1")
        CB = ("C0", "C1")
        for c in range(4):
            order = list(range(nkb - 1, -1, -1))
            nst = len(order)
            chbuf = {}

            def load_chunk(ch, c=c):
                b = ctx["chunk"] % 2
                ctx["chunk"] += 1
                k0 = ch * 16
                k1 = min(nkb, k0 + 16)
                nb = k1 - k0
                gs = list(range(k0 // 4, (k1 - 1) // 4 + 1))
                ADD("pool", lambda e: e.dma_start(out=ktc[b][:, 0:nb * 128], in_=ktscr[which][c][:, k0 * 128:k1 * 128]),
                    reads=[("kts", which, g) for g in gs], writes=[("ktc", b)], chan="ktc%d" % b)
                ADD("pool", lambda e: e.dma_start(out=vc[b][:, 0:nb * 128].rearrange("p (b f) -> p b f", f=128), in_=vscr[which][c][:, k0:k1, :]),
                    reads=[("vts", which, g) for g in gs], writes=[("vc", b)], chan="vc%d" % b)
                chbuf[ch] = b

            def cols_of(kb):
                return max(0, kb - kb0) * 128, T

            def emit_Z(n):
                kb = order[n]
                c0, c1 = cols_of(kb)
                b = chbuf[kb // 16]
                kl = kb % 16
                for hh in range(2):
                    hp = hh * 64
                    mm(psZ[:, hh * 512 + c0:hh * 512 + c1], ktc[b][hp:hp + 64, kl * 128:(kl + 1) * 128], QT[hp:hp + 64, c, c0:c1], True, True,
                       reads=[("ktc", b), ("QT", tp, c)], writes=[("ps", "Z")])

            def emit_expZ(n):
                kb = order[n]
                c0, c1 = cols_of(kb)
                e3 = n % 3
                ADD("act", lambda e: e.activation(out=tmpf[e3][:, :, c0:c1], in_=psZ3[:, :, c0:c1], func=AF.Exp),
                    reads=[("ps", "Z")], writes=[("tmpf", e3)])
                if kb >= kb0:
                    for hh in range(2):
                        ADD("dve", lambda e, hh=hh: e.tensor_tensor(out=tmpf[e3][:, hh, c0:c0 + 128], in0=tmpf[e3][:, hh, c0:c0 + 128], in1=maskM, op=ALU.mult),
                            reads=[("tmpf", e3), ("cst",)], writes=[("tmpf", e3)])

            def emit_ln(n):
                c0, c1 = cols_of(order[n])
                ei, e3 = n % 2, n % 3
                ADD("act", lambda e: e.activation(out=spb[ei][:, :, c0:c1], in_=tmpf[e3][:, :, c0:c1], func=AF.Ln, bias=1.0),
                    reads=[("tmpf", e3)], writes=[("spb", ei)])

            def emit_U(n):
                c0, c1 = cols_of(order[n])
                ei = n % 2
                for hh in range(2):
                    mm(psC[:, hh * 512 + c0:hh * 512 + c1], unegb, spb[ei][:, hh, c0:c1], n == 0, False,
                       reads=[("spb", ei), ("cstb",)], writes=[("ps", "C", hh)])

            def emit_expC(n):
                c0, c1 = cols_of(order[n])
                ei = n % 2
                for hh in range(2):
                    ADD("act", lambda e, hh=hh: e.activation(out=xxb[ei][:, hh, c0:c1], in_=psC[:, hh * 512 + c0:hh * 512 + c1], func=AF.Exp),
                        reads=[("ps", "C", hh)], writes=[("xxb", ei, hh)])

            def emit_L(n):
                c0, c1 = cols_of(order[n])
                if n == nst - 1:
                    return
                ei = n % 2
                for hh in range(2):
                    mm(psC[:, hh * 512 + c0:hh * 512 + c1], lnegb, spb[ei][:, hh, c0:c1], False, False,
                       reads=[("spb", ei), ("cstb",)], writes=[("ps", "C", hh)])

            def emit_w(n):
                c0, c1 = cols_of(order[n])
                ei, e3 = n % 2, n % 3
                ADD("pool", lambda e: e.tensor_tensor(out=wwb[ei][:, :, c0:c1], in0=tmpf[e3][:, :, c0:c1], in1=xxb[ei][:, :, c0:c1], op=ALU.mult),
                    reads=[("tmpf", e3), ("xxb", ei, 0), ("xxb", ei, 1)], writes=[("wwb", ei)])

            def emit_PV(n):
                kb = order[n]
                c0, c1 = cols_of(kb)
                b = chbuf[kb // 16]
                kl = kb % 16
                ei = n % 2
                for hh in range(2):
                    mm(PS["O"][hh * 64:(hh + 1) * 64, c0:c1], vc[b][:, kl * 128 + hh * 64:kl * 128 + (hh + 1) * 64], wwb[ei][:, hh, c0:c1],
                       n == 0, n == nst - 1, reads=[("vc", b), ("wwb", ei)], writes=[("ps", "O")])

            top = (nkb - 1) // 16
            load_chunk(top)
            if top >= 1:
                load_chunk(top - 1)
            emit_Z(0)
            emit_expZ(0)
            if nst > 1:
                emit_Z(1)
            for n in range(nst):
                kb = order[n]
                emit_ln(n)
                emit_U(n)
                if n + 1 < nst:
                    emit_expZ(n + 1)
                if n + 2 < nst:
                    emit_Z(n + 2)
                if n >= 1:
                    emit_PV(n - 1)
                if n >= 1 and kb % 16 == 15 and kb // 16 >= 1:
                    load_chunk(kb // 16 - 1)
                emit_expC(n)
                emit_L(n)
                emit_w(n)
                c0, c1 = cols_of(kb)
                yield (2 * (2 * (c1 - c0) + 224) + 2 * (c1 - c0 + 224)) / 1200.0
            emit_PV(nst - 1)
            ADD("dve", lambda e, c=c: e.tensor_copy(out=mixT[:, 4 + c, 0:T], in_=PS["O"][:, 0:T]),
                reads=[("ps", "O")], writes=[("mixT", tp, 4 + c)])
            yield 0.5

    def genA(t):
        which, tok0, T, ntok, kb0, tp = t["which"], t["tok0"], t["T"], t["ntok"], t["kb0"], t["par"]
        xd, nkd, nvd, ncd = t["xd"], t["nkd"], t["nvd"], t["ncd"]
        hT, mixT, QT = hTs[tp], mixTs[tp], QTs[tp]
        nq = T // 128
        wT = 0.22 * T / 512
        for tb in range(nq):
            r = min(128, ntok - tb * 128)
            b = ctx["xs"] % 2
            ctx["xs"] += 1
            ADD("sp", lambda e, b=b, tb=tb, r=r: e.dma_start(out=xs[b][0:r, :], in_=xd[tok0 + tb * 128:tok0 + tb * 128 + r, :]),
                writes=[("xs", b)], chan="xs%d" % b)
            for g in range(2):
                pn = "G" if g == 0 else "U"
                for q in range(4):
                    dc = g * 4 + q
                    ADD("pe", lambda e, pn=pn, q=q, b=b, dc=dc: e.transpose(PS[pn][:, q * 128:(q + 1) * 128], xs[b][:, dc * 128:(dc + 1) * 128], ident),
                        reads=[("xs", b), ("cst",)], writes=[("ps", pn)])
                ADD("dve", lambda e, pn=pn, g=g, tb=tb: e.tensor_copy(out=hT[:, g * 4:(g + 1) * 4, tb * 128:(tb + 1) * 128],
                                                                       in_=PS[pn][:].rearrange("p (a b) -> p a b", a=4)),
                    reads=[("ps", pn)], writes=[("hT", tp, g * 4 + q) for q in range(4)])
            yield 2.0
        rmsnorm(T, tp, 0)
        yield 8 * wT
        yield from ffn(T, tp, 0)
        sid = 19
        rmsnorm(T, tp, 8)
        yield 8 * wT
        b0, sl0 = get_slab(sid); sid += 1
        b1, sl1 = get_slab(sid); sid += 1
        if t["hist"] == "zero":
            ADD("pool", lambda e: e.memset(u[:, :, 0:30], 0.0), writes=[("u", cc) for cc in range(4)])
        elif t["hist"] == "state":
            ADD("sp", lambda e: e.dma_start(out=sc_sb[0:30, :], in_=sc[:, :]), writes=[("sc_sb",)], chan="sc")
            for cc in range(4):
                ADD("pe", lambda e, cc=cc: e.transpose(PS["X"][:, cc * 32:cc * 32 + 30], sc_sb[0:30, cc * 128:(cc + 1) * 128], ident[0:30, 0:30]),
                    reads=[("sc_sb",), ("cst",)], writes=[("ps", "X")])
            ADD("dve", lambda e: e.tensor_copy(out=u[:, :, 0:30], in_=PS["X"][:, 0:128].rearrange("p (a b) -> p a b", a=4)[:, :, 0:30]),
                reads=[("ps", "X")], writes=[("u", cc) for cc in range(4)])
        for cc in range(4):
            for kc in range(8):
                mm(PS["G"][:, 0:T], sl0[:, kc, cc * 128:(cc + 1) * 128], xn[:, kc, 0:T], kc == 0, kc == 7,
                   reads=[("wslab", b0), ("xn", kc)], writes=[("ps", "G")])
                if kc % 4 == 3:
                    yield 4 * 0.22 * T / 512
            for kc in range(8):
                mm(PS["U"][:, 0:T], sl1[:, kc, cc * 128:(cc + 1) * 128], xn[:, kc, 0:T], kc == 0, kc == 7,
                   reads=[("wslab", b1), ("xn", kc)], writes=[("ps", "U")])
                if kc % 4 == 3:
                    yield 4 * 0.22 * T / 512
            tb_ = cc % 2
            ADD("act", lambda e, tb_=tb_: e.activation(out=eet[tb_][:, 0:T], in_=PS["U"][:, 0:T], func=AF.Exp, scale=-1.0),
                reads=[("ps", "U")], writes=[("eet", tb_)])
            recip1p(tb_, T)
            ADD("dve", lambda e, tb_=tb_, cc=cc: e.tensor_tensor(out=u[:, cc, 30:30 + T], in0=eet[tb_][:, 0:T], in1=PS["G"][:, 0:T], op=ALU.mult),
                reads=[("eet", tb_), ("ps", "G")], writes=[("u", cc)])
            yield 16 * wT
        b2, sl2 = get_slab(sid); sid += 1
        for c in range(4):
            pa = "G" if c % 2 == 0 else "U"
            for kc in range(8):
                mm(PS[pa][:, 0:T], sl2[:, kc, c * 128:(c + 1) * 128], xn[:, kc, 0:T], kc == 0, kc == 7,
                   reads=[("wslab", b2), ("xn", kc)], writes=[("ps", pa)])
                if kc % 4 == 3:
                    yield 4 * 0.22 * T / 512
            ADD("dve", lambda e, pa=pa, c=c: e.tensor_scalar(out=QT[:, c, 0:T], in0=PS[pa][:, 0:T], scalar1=0.125, scalar2=None, op0=ALU.mult),
                reads=[("ps", pa)], writes=[("QT", tp, c)])
            yield 8 * wT
        b3, sl3 = get_slab(sid); sid += 1
        for c in range(4):
            pa = "G" if c % 2 == 0 else "U"
            for kc in range(8):
                mm(PS[pa][:, 0:T], sl3[:, kc, c * 128:(c + 1) * 128], xn[:, kc, 0:T], kc == 0, kc == 7,
                   reads=[("wslab", b3), ("xn", kc)], writes=[("ps", pa)])
                if kc % 4 == 3:
                    yield 4 * 0.22 * T / 512
            ADD("dve", lambda e, pa=pa, c=c: e.tensor_copy(out=KTt[:, c, 0:T], in_=PS[pa][:, 0:T]),
                reads=[("ps", pa)], writes=[("KTt",)])
            yield 8 * wT

        def tokmajor(bslab, sl, outd, also_bf):
            for tb in range(nq):
                r = min(128, ntok - tb * 128)
                ob = ctx["kvo"] % 2
                pX = ("X", "G", "U")[ctx["kvo"] % 3]
                ctx["kvo"] += 1
                for kc in range(8):
                    mm(PS[pX][:, :], xn[:, kc, tb * 128:(tb + 1) * 128], sl[:, kc, :], kc == 0, kc == 7,
                       reads=[("wslab", bslab), ("xn", kc)], writes=[("ps", pX)])
                    if kc % 4 == 3:
                        yield 4 * 0.22 * T / 512
                ADD("dve", lambda e, ob=ob, pX=pX: e.tensor_copy(out=kvo[ob][:], in_=PS[pX][:, :]),
                    reads=[("ps", pX)], writes=[("kvo", ob)])
                if also_bf:
                    ADD("dve", lambda e, ob=ob, tb=tb: e.tensor_copy(out=Vt[:, tb, :], in_=kvo[ob][:]),
                        reads=[("kvo", ob)], writes=[("Vt",)])
                if r > 0:
                    ADD("sp", lambda e, ob=ob, tb=tb, r=r: e.dma_start(out=outd[tok0 + tb * 128:tok0 + tb * 128 + r, :], in_=kvo[ob][0:r, :]),
                        reads=[("kvo", ob)], writes=[("outkv",)], chan="kvo%d" % ob)
                yield 8 * 0.22

        yield from tokmajor(b3, sl3, nkd, False)
        b4, sl4 = get_slab(sid); sid += 1
        yield from tokmajor(b4, sl4, nvd, True)
        kv_store(which, kb0 * 128, T)

        for cc in range(4):
            ADD("dve", lambda e, cc=cc: e.tensor_copy(out=ub[:, cc, 0:30 + T], in_=u[:, cc, 0:30 + T]),
                reads=[("u", cc)], writes=[("ub", cc)])
        for cc in range(4):
            pX = ("X", "G", "U")[cc % 3]
            for k in range(31):
                r = ctx["dg"] % 4
                ctx["dg"] += 1
                wc = 40 + cc * 31 + k
                ADD("dve", lambda e, r=r, wc=wc: e.tensor_scalar(out=dg[r][:], in0=identb, scalar1=par[:, wc:wc + 1], scalar2=None, op0=ALU.mult),
                    reads=[("cstb",), ("par",)], writes=[("dg", r)])
                mm(PS[pX][:, 0:T], dg[r][:], ub[:, cc, k:k + T], k == 0, k == 30,
                   reads=[("dg", r), ("ub", cc)], writes=[("ps", pX)])
                if k % 4 == 3:
                    yield 4 * wT
            ADD("dve", lambda e, cc=cc, pX=pX: e.tensor_scalar(out=yacc[:, cc, 0:T], in0=PS[pX][:, 0:T], scalar1=par[:, 164 + cc:165 + cc], scalar2=None, op0=ALU.add),
                reads=[("ps", pX), ("par",)], writes=[("act", 2 * cc), ("act", 2 * cc + 1)])
            yield 3 * wT
        if t["conv_cols"] is not None:
            cs = t["conv_cols"]
            for cc in range(4):
                ADD("pe", lambda e, cc=cc: e.transpose(PS["X"][0:30, cc * 128:(cc + 1) * 128], u[:, cc, cs:cs + 30], ident),
                    reads=[("u", cc), ("cst",)], writes=[("ps", "X")])
            ADD("dve", lambda e: e.tensor_copy(out=cvo[0:30, :], in_=PS["X"][0:30, :]), reads=[("ps", "X")], writes=[("cvo",)])
            ADD("sp", lambda e: e.dma_start(out=ncd[:, :], in_=cvo[0:30, :]), reads=[("cvo",)], writes=[("outc",)], chan="cvo")
        ADD("dve", lambda e: e.tensor_copy(out=u[:, :, 0:30], in_=u[:, :, T:T + 30]),
            reads=[("u", cc) for cc in range(4)], writes=[("u", cc) for cc in range(4)])
        for cc in range(4):
            ADD("dve", lambda e, cc=cc: e.tensor_copy(out=xn[:, cc, 0:T], in_=yacc[:, cc, 0:T]),
                reads=[("act", 2 * cc), ("act", 2 * cc + 1)], writes=[("xn", cc)])
            ADD("dve", lambda e, cc=cc: e.tensor_tensor(out=xn[:, 4 + cc, 0:T], in0=yacc[:, cc, 0:T], in1=yacc[:, cc, 0:T], op=ALU.mult),
                reads=[("act", 2 * cc), ("act", 2 * cc + 1)], writes=[("xn", 4 + cc)])
        for cc in range(4):
            mm(PS["G"][:, 0:T], onesb, xn[:, cc, 0:T], cc == 0, cc == 3, reads=[("xn", cc), ("cstb",)], writes=[("ps", "G")])
        for cc in range(4):
            mm(PS["U"][:, 0:T], onesb, xn[:, 4 + cc, 0:T], cc == 0, cc == 3, reads=[("xn", 4 + cc), ("cstb",)], writes=[("ps", "U")])
        ADD("dve", lambda e: e.tensor_scalar(out=lnm[:, 0:T], in0=PS["G"][:, 0:T], scalar1=1.0 / 512, scalar2=None, op0=ALU.mult),
            reads=[("ps", "G")], writes=[("lnm",)])
        ADD("dve", lambda e: e.tensor_tensor(out=lnv[:, 0:T], in0=lnm[:, 0:T], in1=lnm[:, 0:T], op=ALU.mult),
            reads=[("lnm",)], writes=[("lnv",)])
        ADD("dve", lambda e: e.scalar_tensor_tensor(out=lnv[:, 0:T], in0=PS["U"][:, 0:T], scalar=1.0 / 512, in1=lnv[:, 0:T], op0=ALU.mult, op1=ALU.subtract),
            reads=[("ps", "U"), ("lnv",)], writes=[("lnv",)])
        ADD("dve", lambda e: e.tensor_scalar(out=lnv[:, 0:T], in0=lnv[:, 0:T], scalar1=0.0, scalar2=None, op0=ALU.max),
            reads=[("lnv",)], writes=[("lnv",)])
        ADD("act", lambda e: e.activation(out=lnv[:, 0:T], in_=lnv[:, 0:T], func=AF.Ln, bias=EPS),
            reads=[("lnv",)], writes=[("lnv",)])
        ADD("act", lambda e: e.activation(out=lnv[:, 0:T], in_=lnv[:, 0:T], func=AF.Exp, scale=-0.5),
            reads=[("lnv",)], writes=[("lnv",)])
        yield 8 * wT
        for cc in range(4):
            eng = "dve"
            ADD(eng, lambda e, cc=cc: e.tensor_tensor(out=yacc[:, cc, 0:T], in0=yacc[:, cc, 0:T], in1=lnm[:, 0:T], op=ALU.subtract),
                reads=[("act", 2 * cc), ("act", 2 * cc + 1), ("lnm",)], writes=[("act", 2 * cc), ("act", 2 * cc + 1)])
            ADD(eng, lambda e, cc=cc: e.tensor_tensor(out=yacc[:, cc, 0:T], in0=yacc[:, cc, 0:T], in1=lnv[:, 0:T], op=ALU.mult),
                reads=[("act", 2 * cc), ("act", 2 * cc + 1), ("lnv",)], writes=[("act", 2 * cc), ("act", 2 * cc + 1)])
            ADD("dve", lambda e, cc=cc: e.tensor_scalar(out=yacc[:, cc, 0:T], in0=yacc[:, cc, 0:T], scalar1=par[:, 168 + cc:169 + cc],
                                                        scalar2=par[:, 172 + cc:173 + cc], op0=ALU.mult, op1=ALU.add),
                reads=[("act", 2 * cc), ("act", 2 * cc + 1), ("par",)], writes=[("act", 2 * cc), ("act", 2 * cc + 1)])
            tb_ = cc % 2
            ADD("act", lambda e, cc=cc, tb_=tb_: e.activation(out=eet[tb_][:, 0:T], in_=yacc[:, cc, 0:T], func=AF.Exp, scale=-1.0),
                reads=[("act", 2 * cc), ("act", 2 * cc + 1)], writes=[("eet", tb_)])
            recip1p(tb_, T)
            ADD("dve", lambda e, cc=cc, tb_=tb_: e.tensor_tensor(out=mixT[:, cc, 0:T], in0=yacc[:, cc, 0:T], in1=eet[tb_][:, 0:T], op=ALU.mult),
                reads=[("act", 2 * cc), ("act", 2 * cc + 1), ("eet", tb_)], writes=[("mixT", tp, cc)])
        yield 1.0

    def genC(t):
        tok0, T, ntok, tp = t["tok0"], t["T"], t["ntok"], t["par"]
        pd, yd = t["pd"], t["yd"]
        hT, mixT = hTs[tp], mixTs[tp]
        nq = T // 128
        wT = 0.22 * T / 512
        sid = 24
        for tb in range(nq):
            r = min(128, ntok - tb * 128)
            b = ctx["ps"] % 2
            ctx["ps"] += 1
            ADD("sp", lambda e, b=b, tb=tb, r=r: e.dma_start(out=pstg[b][0:r, :], in_=pd[tok0 + tb * 128:tok0 + tb * 128 + r, :]),
                writes=[("pstg", b)], chan="pstg%d" % b)
            for kc in range(2):
                ADD("pe", lambda e, b=b, kc=kc: e.transpose(PS["G"][:, kc * 128:(kc + 1) * 128], pstg[b][:, kc * 128:(kc + 1) * 128], ident),
                    reads=[("pstg", b), ("cst",)], writes=[("ps", "G")])
            ADD("dve", lambda e, tb=tb: e.tensor_copy(out=pT[:, :, tb * 128:(tb + 1) * 128], in_=PS["G"][:, 0:256].rearrange("p (a b) -> p a b", a=2)),
                reads=[("ps", "G")], writes=[("pT",)])
        bo = []
        for o in range(2):
            bo.append(get_slab(sid)); sid += 1
        for oc in range(8):
            b, sl = bo[oc // 4]
            pX = ("X", "G", "U")[oc % 3]
            for kc in range(8):
                mm(PS[pX][:, 0:T], sl[:, kc, (oc % 4) * 128:(oc % 4 + 1) * 128], mixT[:, kc, 0:T], kc == 0, kc == 7,
                   reads=[("wslab", b), ("mixT", tp, kc)], writes=[("ps", pX)])
                if kc % 4 == 3:
                    yield 4 * 0.22 * T / 512
            ADD("dve", lambda e, oc=oc, pX=pX: e.tensor_tensor(out=hT[:, oc, 0:T], in0=hT[:, oc, 0:T], in1=PS[pX][:, 0:T], op=ALU.add),
                reads=[("ps", pX), ("hT", tp, oc)], writes=[("hT", tp, oc)])
            yield 8 * wT
        rmsnorm(T, tp, 16)
        yield 8 * wT
        yield from ffn(T, tp, sid)
        sid += 19
        rmsnorm(T, tp, 24)
        yield 8 * wT
        bg = []
        for o in range(2):
            bg.append(get_slab(sid)); sid += 1
        bw, slw = get_slab(sid); sid += 1
        for oc in range(8):
            b, sl = bg[oc // 4]
            for kc in range(8):
                mm(PS["G"][:, 0:T], sl[:, kc, (oc % 4) * 128:(oc % 4 + 1) * 128], xn[:, kc, 0:T], kc == 0, kc == 7,
                   reads=[("wslab", b), ("xn", kc)], writes=[("ps", "G")])
                if kc % 4 == 3:
                    yield 4 * 0.22 * T / 512
            for kc in range(2):
                mm(PS["U"][:, 0:T], slw[:, kc, oc * 128:(oc + 1) * 128], pT[:, kc, 0:T], kc == 0, kc == 1,
                   reads=[("wslab", bw), ("pT",)], writes=[("ps", "U")])
            tb_ = oc % 2
            ADD("act", lambda e, tb_=tb_: e.activation(out=eet[tb_][:, 0:T], in_=PS["G"][:, 0:T], func=AF.Exp, scale=-1.0),
                reads=[("ps", "G")], writes=[("eet", tb_)])
            recip1p(tb_, T)
            ADD("dve", lambda e, tb_=tb_: e.tensor_tensor(out=sgt[tb_][:, 0:T], in0=eet[tb_][:, 0:T], in1=PS["U"][:, 0:T], op=ALU.mult),
                reads=[("eet", tb_), ("ps", "U")], writes=[("sgt", tb_)])
            ADD("dve", lambda e, tb_=tb_, oc=oc: e.tensor_tensor(out=hT[:, oc, 0:T], in0=hT[:, oc, 0:T], in1=sgt[tb_][:, 0:T], op=ALU.add),
                reads=[("sgt", tb_), ("hT", tp, oc)], writes=[("hT", tp, oc)])
            yield 10 * wT
        rmsnorm(T, tp, 32, final=True)
        yield 8 * wT
        for tb in range(nq):
            r = min(128, ntok - tb * 128)
            if r <= 0:
                continue
            b = ctx["ys"] % 2
            ctx["ys"] += 1
            for g in range(2):
                pn = "G" if g == 0 else "U"
                for q in range(4):
                    dc = g * 4 + q
                    ADD("pe", lambda e, pn=pn, q=q, dc=dc, tb=tb: e.transpose(PS[pn][:, q * 128:(q + 1) * 128], hT[:, dc, tb * 128:(tb + 1) * 128], ident),
                        reads=[("hT", tp, dc), ("cst",)], writes=[("ps", pn)])
                ADD("dve", lambda e, pn=pn, b=b, g=g: e.tensor_copy(out=ys[b][:, g * 512:(g + 1) * 512], in_=PS[pn][:, :]),
                    reads=[("ps", pn)], writes=[("ys", b)])
            ADD("sp", lambda e, b=b, tb=tb, r=r: e.dma_start(out=yd[tok0 + tb * 128:tok0 + tb * 128 + r, :], in_=ys[b][0:r, :]),
                reads=[("ys", b)], writes=[("outy",)], chan="ys%d" % b)
            yield 2.0

    tiles = []
    if do_sample:
        for grp in range(4):
            for tb in range(4):
                blk = grp * 4 + tb
                b = ctx["xs"] % 2
                ctx["xs"] += 1
                ADD("sp", lambda e, b=b, blk=blk: e.dma_start(out=xs[b][:, 0:512], in_=ck[blk * 128:(blk + 1) * 128, :]),
                    writes=[("xs", b)], chan="xs%d" % b)
                for c in range(4):
                    ADD("pe", lambda e, b=b, c=c: e.transpose(PS["O"][:, c * 128:(c + 1) * 128], xs[b][:, c * 128:(c + 1) * 128], ident),
                        reads=[("xs", b), ("cst",)], writes=[("ps", "O")])
                ADD("dve", lambda e, tb=tb: e.tensor_copy(out=KTt[:, :, tb * 128:(tb + 1) * 128], in_=PS["O"][:].rearrange("p (a b) -> p a b", a=4)),
                    reads=[("ps", "O")], writes=[("KTt",)])
                ADD("sp", lambda e, b=b, blk=blk: e.dma_start(out=xs[b][:, 512:1024], in_=cv[blk * 128:(blk + 1) * 128, :]),
                    writes=[("xs", b)], chan="xv%d" % b)
                ADD("pool", lambda e, b=b, tb=tb: e.tensor_copy(out=Vt[:, tb, :], in_=xs[b][:, 512:1024]),
                    reads=[("xs", b)], writes=[("Vt",)])
            kv_store("s", grp * 512, 512)
        for b in range(2):
            ADD("pool", lambda e, b=b: e.memset(xs[b][:], 0.0), writes=[("xs", b)])
        tiles.append(dict(which="s", tok0=0, T=128, ntok=TS, kb0=16, xd=x_s, pd=p_s, yd=y_s, nkd=nk_s, nvd=nv_s, ncd=nc_s,
                          hist="state", conv_cols=30 + TS - 30))
    for i in range(n_ptiles):
        last = (i == n_ptiles - 1)
        tiles.append(dict(which="p", tok0=i * 512, T=512, ntok=512, kb0=4 * i, xd=x_p, pd=p_p, yd=y_p, nkd=nk_p, nvd=nv_p, ncd=nc_p,
                          hist="zero" if i == 0 else "carry", conv_cols=(30 + 512 - 30) if last else None))
    for i, t in enumerate(tiles):
        t["par"] = i % 2

    def run_all(g):
        for _ in g:
            pass

    def total_weight(mk):
        saved = dict(ctx)
        real = PH[0]
        PH[0] = DummyProg()
        tot = sum(w for w in mk())
        PH[0] = real
        ctx.clear(); ctx.update(saved)
        return tot

    def chain(*gens):
        for g in gens:
            yield from g

    def interleave(mkA, mkD):
        ta, td = max(total_weight(mkA), 1e-9), max(total_weight(mkD), 1e-9)
        ga, gd = mkA(), mkD()
        da = dd = 0.0
        a_alive = d_alive = True
        while a_alive or d_alive:
            pick_a = a_alive and (not d_alive or da / ta - 0.09 <= dd / td)
            if pick_a:
                try:
                    da += next(ga)
                except StopIteration:
                    a_alive = False
            else:
                try:
                    dd += next(gd)
                except StopIteration:
                    d_alive = False

    NTL = len(tiles)
    if not pipelined:
        for t in tiles:
            run_all(genA(t)); run_all(genB(t)); run_all(genC(t))
    else:
        run_all(genA(tiles[0]))
        for i in range(NTL):
            ctx["recip_act"] = (tiles[i]["kb0"] + tiles[i]["T"] // 128) <= RECIP_ACT_MAX_KB

            def mkD(i=i):
                gens = []
                if i >= 1:
                    gens.append(genC(tiles[i - 1]))
                if i + 1 < NTL:
                    gens.append(genA(tiles[i + 1]))
                return chain(*gens)
            interleave(lambda i=i: genB(tiles[i]), mkD)
        ctx["recip_act"] = True
        run_all(genC(tiles[NTL - 1]))

    P = PH[0]
    P.emit(nc, stack)
    stack.close()
    return nc, P


_CACHE = {}


def _consts():
    c = np.zeros((128, 512), np.float32)
    s = np.arange(128)[:, None]
    j = np.arange(128)[None, :]
    c[:, 0:128] = np.eye(128, dtype=np.float32)
    c[:, 128:256] = -(s >= j).astype(np.float32)
    c[:, 256:384] = -(s < j).astype(np.float32)
    c[:, 384:512] = (s < j).astype(np.float32)
    return c


def _params(inp):
    p = np.zeros((128, NPAR), np.float32)

    def col(v, n):
        return np.ascontiguousarray(np.asarray(v, np.float32).reshape(n, 128).T)

    p[:, 0:8] = col(inp["ffn1_norm"][0], 8)
    p[:, 8:16] = col(inp["mix_norm"][0], 8)
    p[:, 16:24] = col(inp["ffn2_norm"][0], 8)
    p[:, 24:32] = col(inp["ple_norm"][0], 8)
    p[:, 32:40] = col(inp["final_norm"], 8)
    cw = np.asarray(inp["conv_w"][0], np.float32)
    p[:, 40:164] = cw.T.reshape(4, 128, 31).transpose(1, 0, 2).reshape(128, 124)
    p[:, 164:168] = col(inp["conv_b"][0], 4)
    p[:, 168:172] = col(inp["conv_ln_g"][0], 4)
    p[:, 172:176] = col(inp["conv_ln_b"][0], 4)
    return p


def kernel(**inp):
    inp = {k: np.asarray(v) for k, v in inp.items()}
    if "nc" not in _CACHE:
        _CACHE["nc"] = build_nc()[0]
    nc = _CACHE["nc"]
    consts = _consts()
    params = _params(inp)
    shared = {
        "f1gu": np.ascontiguousarray(inp["ffn1_w_gu"][0]), "f1dn": np.ascontiguousarray(inp["ffn1_w_down"][0]),
        "win": np.ascontiguousarray(inp["w_in"][0]), "wout": np.ascontiguousarray(inp["w_out"][0]),
        "f2gu": np.ascontiguousarray(inp["ffn2_w_gu"][0]), "f2dn": np.ascontiguousarray(inp["ffn2_w_down"][0]),
        "pgate": np.ascontiguousarray(inp["ple_gate_w"][0]), "plew": np.ascontiguousarray(inp["ple_w"][0]),
        "params": params, "consts": consts,
    }
    in_maps = []
    for c in range(8):
        m = dict(shared)
        m["x_p"] = np.ascontiguousarray(inp["x_prompt"][c])
        m["x_s"] = np.ascontiguousarray(inp["x_sample"][c])
        m["p_p"] = np.ascontiguousarray(inp["p_prompt"][0, c])
        m["p_s"] = np.ascontiguousarray(inp["p_sample"][0, c])
        m["ck"] = np.ascontiguousarray(inp["cache_k"][0, c].reshape(PAST, 512))
        m["cv"] = np.ascontiguousarray(inp["cache_v"][0, c].reshape(PAST, 512))
        m["sc"] = np.ascontiguousarray(inp["state_conv"][0, c])
        in_maps.append(m)
    res = run_bass_kernel_spmd(nc, in_maps, core_ids=list(range(8)))
    R = res.results

    def g(name):
        return np.stack([np.asarray(R[c][name], np.float32) for c in range(8)])

    y_prompt = g("y_p")
    y_sample = g("y_s")
    nk_p = g("nk_p").reshape(1, 8, SEQ, 8, 64)
    nv_p = g("nv_p").reshape(1, 8, SEQ, 8, 64)
    nc_p = g("nc_p").reshape(1, 8, 30, 512)
    nk_s = g("nk_s").reshape(1, 8, TS, 8, 64)
    nv_s = g("nv_s").reshape(1, 8, TS, 8, 64)
    nc_s = g("nc_s").reshape(1, 8, 30, 512)
    return (y_prompt, y_sample, nk_p, nv_p, nc_p, nk_s, nv_s, nc_s)
```
